# Optimizing a Trainium2 kernel written in Bass

```python
import jax
import jax.numpy as jnp
from jax import lax
import numpy as np

D_MODEL = 1024
BATCH = 4
SEQ = 8192
DEPTH = 1

ATT_GROUPS = ((128, 1), (512, 4), (2048, 16))
ATT_HEADS_PER_GROUP = 4
ATT_HEADS = ATT_HEADS_PER_GROUP * len(ATT_GROUPS)
ATT_HEAD_DIM = 128
ATT_BLOCK = 128
ATT_W = ATT_HEADS * ATT_HEAD_DIM
ATT_OUT_W = ATT_HEADS_PER_GROUP * ATT_HEAD_DIM
RET_HEADS = 4
RET_QK_DIM = D_MODEL // RET_HEADS
RET_V_DIM = 2 * D_MODEL // RET_HEADS
RET_QK_W = RET_HEADS * RET_QK_DIM
RET_V_W = RET_HEADS * RET_V_DIM
RET_CHUNK = 128
D_FF = 4 * D_MODEL
IN_SPLITS = (ATT_W, ATT_W, ATT_W, RET_QK_W, RET_QK_W, RET_V_W, RET_V_W, D_MODEL, D_MODEL)
IN_W = sum(IN_SPLITS)
EPS = 1e-6

kernel_name = 'hybrid_dilated_attention_retention_block'


def rmsnorm(x, g):
    xf = x.astype(jnp.float32)
    y = xf * lax.rsqrt(jnp.mean(xf * xf, axis=-1, keepdims=True) + EPS)
    return (y * g.astype(jnp.float32)).astype(x.dtype)


def alibi_slopes(n_heads):
    return jnp.asarray(2.0 ** (-8.0 * np.arange(1, n_heads + 1, dtype=np.float32) / n_heads), dtype=jnp.float32)


def dilated_group(q, k, v, slopes, window, dilation):
    B, S, H, Dh = q.shape
    span = dilation * ATT_BLOCK
    s_pad = -(-S // span) * span
    L = s_pad // dilation
    nb = L // ATT_BLOCK
    reach = window // dilation

    def to_blocks(t):
        t = jnp.pad(t, ((0, 0), (0, s_pad - S), (0, 0), (0, 0)))
        t = t.reshape(B, L, dilation, H, Dh).transpose(0, 2, 3, 1, 4)
        return t.reshape(B, dilation, H, nb, ATT_BLOCK, Dh)

    def with_prev(t):
        prev = jnp.pad(t, ((0, 0), (0, 0), (0, 0), (1, 0), (0, 0), (0, 0)))[:, :, :, :-1]
        return jnp.concatenate([prev, t], axis=4)

    qb = to_blocks(q)
    kb = with_prev(to_blocks(k))
    vb = with_prev(to_blocks(v))
    s = jnp.einsum('brhnqd,brhnkd->brhnqk', qb, kb, preferred_element_type=jnp.float32) * (Dh ** -0.5)
    qi = jnp.arange(ATT_BLOCK)[:, None]
    kj = jnp.arange(2 * ATT_BLOCK)[None, :]
    dist = ATT_BLOCK + qi - kj
    blk = jnp.arange(nb)[:, None, None]
    valid = (dist >= 0) & (dist <= reach) & ((blk - 1) * ATT_BLOCK + kj >= 0)
    bias = -slopes[:, None, None] * (dist * dilation).astype(jnp.float32)
    s = jnp.where(valid[None, None, None], s + bias[None, None, :, None], -jnp.inf)
    m = jnp.max(s, axis=-1)
    p = jnp.exp(s - m[..., None])
    den = jnp.sum(p, axis=-1)
    o = jnp.einsum('brhnqk,brhnkd->brhnqd', p, vb.astype(jnp.float32)) / den[..., None]
    lse = m + jnp.log(den)
    o = o.reshape(B, dilation, H, L, Dh).transpose(0, 3, 1, 2, 4).reshape(B, s_pad, H, Dh)[:, :S]
    lse = lse.reshape(B, dilation, H, L).transpose(0, 3, 1, 2).reshape(B, s_pad, H)[:, :S]
    return o, lse


def retention(q, k, v):
    B, S, H, dk = q.shape
    dv = v.shape[-1]
    C = RET_CHUNK
    N = S // C
    log_g = jnp.log(1.0 - 2.0 ** (-5.0 - jnp.arange(H, dtype=jnp.float32)))
    idx = jnp.arange(C, dtype=jnp.float32)
    diff = idx[:, None] - idx[None, :]
    decay = jnp.where(diff >= 0, jnp.exp(log_g[:, None, None] * jnp.maximum(diff, 0.0)), 0.0)
    xi = jnp.exp(log_g[None, :] * (idx[:, None] + 1.0))
    zeta = jnp.exp(log_g[None, :] * (C - 1.0 - idx[:, None]))
    g_chunk = jnp.exp(log_g * C)
    qc = q.astype(jnp.float32).reshape(B, N, C, H, dk)
    kc = (k.astype(jnp.float32) * (dk ** -0.5)).reshape(B, N, C, H, dk)
    vc = v.astype(jnp.float32).reshape(B, N, C, H, dv)
    s = jnp.einsum('bnqhd,bnkhd->bnhqk', qc, kc) * decay[None, None]
    inner = jnp.einsum('bnhqk,bnkhv->bnqhv', s, vc)
    kz = kc * zeta[None, None, :, :, None]

    def step(state, xs):
        q_i, kz_i, v_i = xs
        cross = jnp.einsum('bqhd,bhdv->bqhv', q_i, state)
        state = state * g_chunk[None, :, None, None] + jnp.einsum('bkhd,bkhv->bhdv', kz_i, v_i)
        return state, cross

    state0 = jnp.zeros((B, H, dk, dv), jnp.float32)
    _, cross = lax.scan(step, state0, (qc.transpose(1, 0, 2, 3, 4), kz.transpose(1, 0, 2, 3, 4), vc.transpose(1, 0, 2, 3, 4)))
    cross = cross.transpose(1, 0, 2, 3, 4) * xi[None, None, :, :, None]
    return (inner + cross).reshape(B, S, H, dv)


def head_groupnorm(o, g, b):
    B, S, H, dv = o.shape
    mu = jnp.mean(o, axis=-1, keepdims=True)
    var = jnp.mean(jnp.square(o - mu), axis=-1, keepdims=True)
    y = ((o - mu) * lax.rsqrt(var + EPS)).reshape(B, S, H * dv)
    return y * g.astype(jnp.float32) + b.astype(jnp.float32)


def setup_inputs(seed: int = 0) -> dict:
    key = jax.random.key(seed)
    ks = jax.random.split(key, 14)

    def nrm(k, shape, scale):
        return scale * jax.random.normal(k, shape, jnp.float32)

    return {
        'x': nrm(ks[0], (BATCH, SEQ, D_MODEL), 1.0),
        'norm1_g': 1.0 + nrm(ks[1], (DEPTH, D_MODEL), 0.02),
        'w_in': nrm(ks[2], (DEPTH, D_MODEL, IN_W), D_MODEL ** -0.5),
        'q_norm_g': 1.0 + nrm(ks[3], (DEPTH, ATT_HEADS, ATT_HEAD_DIM), 0.02),
        'k_norm_g': 1.0 + nrm(ks[4], (DEPTH, ATT_HEADS, ATT_HEAD_DIM), 0.02),
        'ret_gn_g': 1.0 + nrm(ks[5], (DEPTH, RET_V_W), 0.02),
        'ret_gn_b': nrm(ks[6], (DEPTH, RET_V_W), 0.02),
        'w_proj_a': nrm(ks[7], (DEPTH, ATT_OUT_W, D_MODEL), ATT_OUT_W ** -0.5),
        'w_proj_b': nrm(ks[8], (DEPTH, RET_V_W, D_MODEL), RET_V_W ** -0.5),
        'w_out': nrm(ks[9], (DEPTH, D_MODEL, D_MODEL), D_MODEL ** -0.5),
        'norm2_g': 1.0 + nrm(ks[10], (DEPTH, D_MODEL), 0.02),
        'w_up': nrm(ks[11], (DEPTH, D_MODEL, D_FF), D_MODEL ** -0.5),
        'w_down': nrm(ks[12], (DEPTH, D_FF, D_MODEL), D_FF ** -0.5),
    }


def reference(x, norm1_g, w_in, q_norm_g, k_norm_g, ret_gn_g, ret_gn_b, w_proj_a, w_proj_b, w_out, norm2_g, w_up, w_down):
    B, S, _ = x.shape
    slopes = alibi_slopes(ATT_HEADS)
    bounds = np.cumsum((0,) + IN_SPLITS).tolist()
    for l in range(DEPTH):
        xn = rmsnorm(x, norm1_g[l])
        wl = w_in[l]
        qa, ka, va, qr, kr, vr, gr, gate_a, gate_b = [xn @ wl[:, bounds[i]:bounds[i + 1]] for i in range(len(IN_SPLITS))]
        qa = rmsnorm(qa.reshape(B, S, ATT_HEADS, ATT_HEAD_DIM), q_norm_g[l])
        ka = rmsnorm(ka.reshape(B, S, ATT_HEADS, ATT_HEAD_DIM), k_norm_g[l])
        va = va.reshape(B, S, ATT_HEADS, ATT_HEAD_DIM)
        outs, lses = [], []
        for gi, (window, dilation) in enumerate(ATT_GROUPS):
            hs = slice(gi * ATT_HEADS_PER_GROUP, (gi + 1) * ATT_HEADS_PER_GROUP)
            o, lse = dilated_group(qa[:, :, hs], ka[:, :, hs], va[:, :, hs], slopes[hs], window, dilation)
            outs.append(o)
            lses.append(lse)
        alpha = jax.nn.softmax(jnp.stack(lses, axis=0), axis=0)
        o_a = jnp.sum(alpha[..., None] * jnp.stack(outs, axis=0), axis=0).reshape(B, S, ATT_OUT_W).astype(x.dtype)
        o_r = retention(qr.reshape(B, S, RET_HEADS, RET_QK_DIM), kr.reshape(B, S, RET_HEADS, RET_QK_DIM),
                        vr.reshape(B, S, RET_HEADS, RET_V_DIM))
        o_r = (head_groupnorm(o_r, ret_gn_g[l], ret_gn_b[l]) * jax.nn.silu(gr.astype(jnp.float32))).astype(x.dtype)
        y = jax.nn.sigmoid(gate_a) * (o_a @ w_proj_a[l]) + jax.nn.sigmoid(gate_b) * (o_r @ w_proj_b[l])
        x = x + y @ w_out[l]
        xn2 = rmsnorm(x, norm2_g[l])
        x = x + jnp.square(jax.nn.relu(xn2 @ w_up[l])) @ w_down[l]
    return x
```

```python
import contextlib
import numpy as np
import concourse.bass as bass
import concourse.mybir as mybir
from concourse.bass_utils import run_bass_kernel_spmd

F32 = mybir.dt.float32
BF16 = mybir.dt.bfloat16
AF = mybir.ActivationFunctionType
ALU = mybir.AluOpType

COMPUTE = ("pe", "act", "dve", "pool")
QUEUES = ("sp",)
EPS = 1e-6
NEG = -30000.0


def MM(out, lhsT, rhs, start=True, stop=True):
    return lambda e: e.matmul(out, lhsT=lhsT, rhs=rhs, start=start, stop=stop)


def TR(out, in_, identity):
    return lambda e: e.transpose(out=out, in_=in_, identity=identity)


def ACTF(out, in_, func, **kw):
    return lambda e: e.activation(out=out, in_=in_, func=func, **kw)


def ACP(out, in_):
    return lambda e: e.copy(out=out, in_=in_)


def CP(out, in_):
    return lambda e: e.tensor_copy(out=out, in_=in_)


def TT(out, in0, in1, op):
    return lambda e: e.tensor_tensor(out=out, in0=in0, in1=in1, op=op)


def TS(out, in0, s1, s2, op0, op1=None):
    if op1 is None:
        return lambda e: e.tensor_scalar(out=out, in0=in0, scalar1=s1, scalar2=None, op0=op0)
    return lambda e: e.tensor_scalar(out=out, in0=in0, scalar1=s1, scalar2=s2, op0=op0, op1=op1)


def STT(out, in0, scalar, in1, op0, op1):
    return lambda e: e.scalar_tensor_tensor(out=out, in0=in0, scalar=scalar, in1=in1, op0=op0, op1=op1)


def MS(ap, val):
    return lambda e: e.memset(ap, val)


def RCP(out, in_):
    return lambda e: e.reciprocal(out=out, in_=in_)


def DMA(out, in_, **kw):
    return lambda e: e.dma_start(out=out, in_=in_, **kw)


class Buf:
    __slots__ = ("name", "lws", "reads")

    def __init__(self, name=""):
        self.name = name
        self.lws = {}
        self.reads = []


class DmaSem:
    __slots__ = ("sem", "val", "name")

    def __init__(self, name):
        self.name = name
        self.sem = None
        self.val = 0


class Prog:
    def __init__(self, nc):
        self.nc = nc
        self.ops = {e: [] for e in COMPUTE + QUEUES}
        self.seen = {e: {} for e in COMPUTE + QUEUES}
        self.dsems = []
        self.stack = contextlib.ExitStack()

    def dsem(self, name):
        d = DmaSem(name)
        self.dsems.append(d)
        return d

    def _need(self, eng, dep, waits, kind):
        if dep[0] == "eng":
            _, e2, idx = dep
            if e2 == eng and (eng == "pe" or kind != "raw"):
                return
            key = ("eng", e2)
            if self.seen[eng].get(key, -1) >= idx:
                return
            self.seen[eng][key] = idx
            self.ops[e2][idx][2] = True
            waits.append(dep)
        else:
            _, ds, val = dep
            key = ("dma", id(ds))
            if self.seen[eng].get(key, -1) >= val:
                return
            self.seen[eng][key] = val
            waits.append(dep)

    def _deps(self, eng, reads, writes):
        waits = []
        for b in reads:
            for d in b.lws.values():
                self._need(eng, d, waits, "raw")
        for b in writes:
            for d in b.lws.values():
                self._need(eng, d, waits, "waw")
            for r in b.reads:
                self._need(eng, r, waits, "war")
        return waits

    def _commit(self, me, key, reads, writes):
        for b in reads:
            b.reads.append(me)
        for b in writes:
            b.lws[key] = me
            b.reads = []

    def op(self, eng, fn, reads=(), writes=()):
        waits = self._deps(eng, reads, writes)
        idx = len(self.ops[eng])
        self.ops[eng].append([waits, fn, False, None])
        self._commit(("eng", eng, idx), eng, reads, writes)

    def dma(self, fn, ds, reads=(), writes=(), queue="sp"):
        waits = self._deps(queue, reads, writes)
        ds.val += 16
        self.ops[queue].append([waits, fn, False, ds])
        self._commit(("dma", ds, ds.val), ("dma", id(ds)), reads, writes)

    def barrier(self):
        lasts = {}
        for e in COMPUTE:
            for i in range(len(self.ops[e]) - 1, -1, -1):
                o = self.ops[e][i]
                if o[1] is not None and o[3] is None:
                    lasts[e] = i
                    break
        dvals = [(d, d.val) for d in self.dsems if d.val > 0]
        for eng in COMPUTE + QUEUES:
            waits = []
            for e2, idx in lasts.items():
                if e2 == eng and eng == "pe":
                    continue
                self._need(eng, ("eng", e2, idx), waits, "raw")
            for d, v in dvals:
                self._need(eng, ("dma", d, v), waits, "raw")
            self.ops[eng].append([waits, None, False, None])

    def wait_bufs(self, eng, bufs):
        waits = []
        for b in bufs:
            for d in b.lws.values():
                self._need(eng, d, waits, "raw")
        self.ops[eng].append([waits, None, False, None])

    def flush(self):
        nc = self.nc
        st = self.stack
        esem = {e: st.enter_context(nc.semaphore("s_" + e)) for e in COMPUTE}
        for d in self.dsems:
            d.sem = st.enter_context(nc.semaphore("d_" + d.name))
        sig = {}
        for e in COMPUTE:
            c = 0
            arr = []
            for o in self.ops[e]:
                if o[2]:
                    c += 1
                arr.append(c)
            sig[e] = arr
        block = st.enter_context(nc.Block())

        def emit(ename):
            def body(eng):
                for waits, fn, signaling, ds in self.ops[ename]:
                    for w in waits:
                        if w[0] == "eng":
                            eng.wait_ge(esem[w[1]], sig[w[1]][w[2]])
                        else:
                            eng.wait_ge(w[1].sem, w[2])
                    if fn is None:
                        continue
                    ins = fn(eng)
                    if ds is not None:
                        ins.then_inc(ds.sem, 16)
                    elif signaling:
                        ins.then_inc(esem[ename], 1)
            return body

        block.tensor(emit("pe"))
        block.scalar(emit("act"))
        block.vector(emit("dve"))
        block.gpsimd(emit("pool"))
        block.sync(emit("sp"))
        st.close()


D = 1024
HALF = 4096
IN_W = 12800
C_QA, C_KA, C_VA, C_QR, C_KR, C_VR, C_GR, C_GA, C_GB = 0, 1536, 3072, 4608, 5632, 6656, 8704, 10752, 11776
DIL = (1, 4, 16)
ARENA = 212480


def _host_consts():
    c = {}
    c["ident"] = np.eye(128, dtype=np.float32)
    slopes = 2.0 ** (-8.0 * np.arange(1, 13, dtype=np.float32) / 12)
    kj = np.arange(128)[:, None]
    qi = np.arange(128)[None, :]
    ab = np.empty((128, 12, 256), np.float32)
    for h in range(12):
        dil = DIL[h // 4]
        dprev = (128 + qi - kj).astype(np.float32)
        dcur = (qi - kj).astype(np.float32)
        ab[:, h, 0:128] = np.where(kj >= qi, -slopes[h] * dprev * dil, NEG)
        ab[:, h, 128:256] = np.where(kj <= qi, -slopes[h] * dcur * dil, NEG)
    c["abias"] = ab
    lg = np.log(1.0 - 2.0 ** (-5.0 - np.arange(4, dtype=np.float64)))
    p = np.arange(128, dtype=np.float64)
    dp = np.zeros((128, 4, 128), np.float64)
    for h in range(4):
        dp[:, h, :] = np.where(kj <= qi, np.exp(-lg[h] * (p[:, None] + 1.0)) / 16.0, 0.0)
    c["dpm"] = dp.astype(np.float32)
    xi = np.exp(lg[:, None] * (p[None, :] + 1.0))
    c["xibc"] = np.broadcast_to(np.tile(xi, (1, 4))[None], (128, 4, 512)).astype(np.float32).copy()
    c["zeta16"] = (np.exp(lg[None, :] * (127.0 - p[:, None])) / 16.0).astype(np.float32)
    z5 = np.empty((128, 4, 4), np.float64)
    for sub in range(4):
        z5[:, sub, :] = np.exp(lg[None, :] * (511.0 - (sub * 128 + p[:, None]))) / 16.0
    c["zeta512"] = z5.astype(np.float32)
    c["g128"] = [float(np.exp(lg[h] * 128.0)) for h in range(4)]
    c["g512"] = [float(np.exp(lg[h] * 512.0)) for h in range(4)]
    return c


_HC = _host_consts()


def build_nc(debug=False):
    nc = bass.Bass("TRN2", target_bir_lowering=False)

    def din(name, shape):
        return nc.dram_tensor(name, list(shape), F32, kind="ExternalInput").ap()

    def dscr(name, shape, dt, out=False):
        return nc.dram_tensor(name, list(shape), dt, kind="ExternalOutput" if out else "Internal").ap()

    x_main = din("x_main", [HALF, D])
    x_prev = din("x_prev", [HALF, D])
    hflag = din("hflag", [128, 128])
    norm1_g = din("norm1_g", [1, D])
    norm2_g = din("norm2_g", [1, D])
    w_in = din("w_in", [D, IN_W])
    q_norm_g = din("q_norm_g", [12, 128])
    k_norm_g = din("k_norm_g", [12, 128])
    ret_gn_g = din("ret_gn_g", [1, 2048])
    ret_gn_b = din("ret_gn_b", [1, 2048])
    w_proj_a = din("w_proj_a", [512, D])
    w_proj_b = din("w_proj_b", [2048, D])
    w_out = din("w_out", [D, D])
    w_up = din("w_up", [D, 4096])
    w_down = din("w_down", [4096, D])
    c_ident = din("c_ident", [128, 128])
    c_abias = din("c_abias", [128, 12, 256])
    c_dpm = din("c_dpm", [128, 4, 128])
    c_xibc = din("c_xibc", [128, 4, 512])
    c_zeta16 = din("c_zeta16", [128, 4])
    c_zeta512 = din("c_zeta512", [128, 4, 4])
    out = nc.dram_tensor("out", [HALF, D], F32, kind="ExternalOutput").ap()

    w_in_b = dscr("w_in_b", [D, IN_W], BF16)
    wa_b = dscr("wa_b", [512, D], BF16)
    wb_b = dscr("wb_b", [2048, D], BF16)
    wout_b = dscr("wout_b", [D, D], BF16)
    wup_b = dscr("wup_b", [D, 4096], BF16)
    wdown_b = dscr("wdown_b", [4096, D], BF16)
    w_att_b = dscr("w_att_b", [4, 128, 9216], BF16)
    dbg = bool(debug)
    xnT_d = dscr("xnT_d", [D, 2 * HALF], BF16, out=dbg)
    oaT_d = dscr("oaT_d", [512, HALF], BF16, out=dbg)
    dbg_s = dscr("dbg_s", [128, 8 * 512], F32, out=True) if dbg else None
    dbg_or = dscr("dbg_or", [128, 16 * 512], BF16, out=True) if dbg else None
    dbg_y = dscr("dbg_y", [128, 8 * 512], BF16, out=True) if dbg else None
    dbg_sg = dscr("dbg_sg", [128, 16 * 512], BF16, out=True) if dbg else None

    P = Prog(nc)
    st = P.stack
    arena = st.enter_context(nc.sbuf_tensor("arena", [128, ARENA // 2], BF16))
    pbt = [st.enter_context(nc.psum_tensor("pb%d" % i, [128, 512], F32)) for i in range(8)]
    pb = [t[:] for t in pbt]
    pbh = [t[:].bitcast(BF16) for t in pbt]
    PB = [Buf("pb%d" % i) for i in range(8)]
    letters = "abcdefg"

    class Carver:
        def __init__(self, base):
            self.off = base

        def __call__(self, shape, dt):
            n = int(np.prod(shape))
            esz = 4 if dt == F32 else 2
            nb = (n * esz + 31) // 32 * 32
            a = arena[:, self.off // 2: self.off // 2 + n * esz // 2]
            self.off += nb
            assert self.off <= ARENA, ("arena overflow", self.off)
            if dt == F32:
                a = a.bitcast(F32)
            if len(shape) > 1:
                names = " ".join(letters[i] for i in range(len(shape)))
                kw = {letters[i]: int(shape[i]) for i in range(len(shape))}
                a = a.rearrange("p (%s) -> p %s" % (names, names), **kw)
            return a

    def finish():
        P.barrier()
        P.flush()
        return nc

    G = Carver(0)
    ident = G([128], BF16)
    ones = G([128], BF16)
    ones_h = G([128], BF16)
    epsc = G([1], F32)
    mhalf = G([4], F32)
    gq = G([12], F32)
    gk = G([12], F32)
    gng = G([16], F32)
    gnb = G([16], F32)
    zeta16 = G([4], F32)
    zeta512 = G([4, 4], F32)
    tmpc = G([128], F32)
    tmpc2 = G([128], F32)
    B_const = Buf("const")
    B_tmpc = Buf("tmpc")
    B_tmpc2 = Buf("tmpc2")
    ds_c = P.dsem("const")
    dsn = [0]

    def newds():
        dsn[0] += 1
        return P.dsem("c%d" % dsn[0])
    GBASE = (G.off + 63) // 64 * 64

    P.dma(DMA(tmpc, c_ident), newds(), writes=[B_tmpc])
    P.dma(DMA(tmpc2, hflag), newds(), writes=[B_tmpc2])
    P.dma(DMA(gq, q_norm_g.rearrange("h d -> d h"), allow_slow_non_contiguous=True), ds_c, writes=[B_const])
    P.dma(DMA(gk, k_norm_g.rearrange("h d -> d h"), allow_slow_non_contiguous=True), ds_c, writes=[B_const])
    P.dma(DMA(gng, ret_gn_g.rearrange("o (c p) -> p (o c)", p=128), allow_slow_non_contiguous=True), ds_c, writes=[B_const])
    P.dma(DMA(gnb, ret_gn_b.rearrange("o (c p) -> p (o c)", p=128), allow_slow_non_contiguous=True), ds_c, writes=[B_const])
    P.dma(DMA(zeta16, c_zeta16), ds_c, writes=[B_const])
    P.dma(DMA(zeta512, c_zeta512), ds_c, writes=[B_const])
    P.op("dve", CP(ident, tmpc), reads=[B_tmpc], writes=[B_const])
    P.op("dve", CP(ones_h, tmpc2), reads=[B_tmpc2], writes=[B_const])
    P.op("dve", MS(ones, 1.0), writes=[B_const])
    P.op("dve", MS(epsc, EPS), writes=[B_const])
    P.op("dve", MS(mhalf, -0.5), writes=[B_const])
    P.op("dve", TS(gq, gq, float(128.0 ** -0.5), None, ALU.mult), reads=[B_const], writes=[B_const])

    C = Carver(GBASE)
    S32 = C([8, 512], F32)
    Sbf = C([8, 512], BF16)
    B_S32 = [Buf("S32_%d" % i) for i in range(8)]
    B_Sbf = [Buf("Sbf_%d" % i) for i in range(8)]
    SBASE = (C.off + 63) // 64 * 64
    C = Carver(SBASE)
    wkv = C([8, 3072], BF16)
    B_wkv = Buf("wkv")
    ds_wkv = P.dsem("wkv")
    w_in_f = w_in.rearrange("(k p) c -> p k c", p=128)
    for i in range(6):
        P.dma(DMA(wkv[:, :, i * 512:(i + 1) * 512], w_in_f[:, :, C_KR + i * 512:C_KR + (i + 1) * 512]),
              ds_wkv, writes=[B_wkv], queue="pool")
    WB = {}
    cast_jobs = []
    for name, src, dst, rows, nch, c0, c1 in (
        ("w_in_rest", w_in, w_in_b, D, 8, C_QR, IN_W),
        ("wa", w_proj_a, wa_b, 512, 1, 0, D), ("wb", w_proj_b, wb_b, 2048, 2, 0, D),
        ("wout", w_out, wout_b, D, 1, 0, D), ("wup", w_up, wup_b, D, 4, 0, 4096), ("wdown", w_down, wdown_b, 4096, 4, 0, D),
    ):
        b = Buf(name)
        ds = P.dsem("w_" + name)
        step = rows // nch
        for i in range(nch):
            cast_jobs.append((dst[i * step:(i + 1) * step, c0:c1], src[i * step:(i + 1) * step, c0:c1], ds, b))
        WB[name] = b

    def issue_casts(n):
        for _ in range(n):
            if cast_jobs:
                dst_, src_, ds_, b_ = cast_jobs.pop(0)
                P.dma(DMA(dst_, src_), ds_, writes=[b_], queue="pool")

    evac_rr = [0]

    def evac_copy(out_ap, in_ap, reads, writes):
        evac_rr[0] += 1
        if evac_rr[0] % 2:
            P.op("act", ACP(out_ap, in_ap), reads=reads, writes=writes)
        else:
            P.op("dve", CP(out_ap, in_ap), reads=reads, writes=writes)

    def norm_T(xin_, B_xin_, gbc, B_gbc, junk_, B_junk_, ss, B_ss, xnb_, B_xnb_, bank, dst, B_dst):
        P.op("act", ACTF(junk_, xin_, AF.Square, accum_out=ss), reads=[B_xin_], writes=[B_junk_, B_ss])
        P.op("act", ACTF(ss, ss, AF.Ln, bias=epsc, scale=1.0 / D), reads=[B_ss, B_const], writes=[B_ss])
        P.op("act", ACTF(ss, ss, AF.Exp, scale=-0.5), reads=[B_ss], writes=[B_ss])
        P.op("dve", STT(xnb_, xin_, ss, gbc, ALU.mult, ALU.mult), reads=[B_xin_, B_ss, B_gbc], writes=[B_xnb_])
        for kc in range(8):
            P.op("pe", TR(pbh[bank][:, kc * 128:(kc + 1) * 128], xnb_[:, kc * 128:(kc + 1) * 128], ident),
                 reads=[B_xnb_, B_const], writes=[PB[bank]])
        evac_copy(dst, pbh[bank][:, 0:1024].rearrange("p (k t) -> p k t", k=8), [PB[bank]], [B_dst])

    xt = [C([8, 512], BF16) for _ in range(2)]
    B_xt = [Buf() for _ in range(2)]
    ds_xt = [P.dsem("r0x%d" % i) for i in range(2)]
    Kz = [C([4, 1024], BF16) for _ in range(2)]
    B_Kz = [Buf() for _ in range(2)]
    Vr = [C([4, 2048], BF16) for _ in range(2)]
    B_Vr = [Buf() for _ in range(2)]
    g1bc = C([1, D], F32)
    B_g1 = Buf("g1bc")
    P.dma(DMA(g1bc, norm1_g.partition_broadcast(128)), newds(), writes=[B_g1])
    xin = [C([D], F32) for _ in range(6)]
    B_xin = [Buf("xin%d" % i) for i in range(6)]
    ds_xin = [P.dsem("xin%d" % i) for i in range(6)]
    junk = C([D], BF16)
    B_junk = Buf("junk")
    ssx = [C([1], F32) for _ in range(8)]
    B_ssx = [Buf() for _ in range(8)]
    xnb = [C([D], BF16) for _ in range(8)]
    B_xnb = [Buf() for _ in range(8)]
    stg = [C([8, 512], BF16) for _ in range(2)]
    B_stg = [Buf() for _ in range(2)]
    ds_stg = [P.dsem("stg%d" % i) for i in range(2)]
    XN = [Buf("xnT_d%d" % i) for i in range(16)]
    xnT_v = xnT_d.rearrange("(k p) t -> p k t", p=128)
    w_in_v = w_in_b.rearrange("(k p) c -> p k c", p=128)
    for hc in range(8):
        P.op("pool", MS(S32[:, hc, :], 0.0), writes=[B_S32[hc]])
    WATT = [Buf("watt%d" % j) for j in range(4)]
    ds_watt = [P.dsem("watt%d" % j) for j in range(4)]
    def watt_cast(j):
        for i3, c0 in enumerate((C_QA, C_KA, C_VA)):
            wsrc = w_in_f[:, :, c0:c0 + 1536].rearrange("p k (g j c) -> p k g j c", g=3, j=4)[:, :, :, j, :]
            for kc in range(8):
                o0 = (i3 * 8 + kc) * 384
                P.dma(DMA(w_att_b[j][:, o0:o0 + 384].rearrange("p (g c) -> p g c", g=3), wsrc[:, kc]), ds_watt[j],
                      reads=[XN[2 * j + 4]], writes=[WATT[j]], queue="pool")


    def x_part1(tt):
        src = x_prev if tt < 32 else x_main
        r0 = (tt % 32) * 128
        sl = tt % 6
        n2 = tt % 8
        P.dma(DMA(xin[sl], src[r0:r0 + 128, :]), ds_xin[sl], writes=[B_xin[sl]])
        P.op("act", ACTF(junk, xin[sl], AF.Square, accum_out=ssx[n2]), reads=[B_xin[sl]], writes=[B_junk, B_ssx[n2]])
        P.op("act", ACTF(ssx[n2], ssx[n2], AF.Ln, bias=epsc, scale=1.0 / D), reads=[B_ssx[n2], B_const], writes=[B_ssx[n2]])
        P.op("act", ACTF(ssx[n2], ssx[n2], AF.Exp, scale=-0.5), reads=[B_ssx[n2]], writes=[B_ssx[n2]])
        P.op("dve", STT(xnb[n2], xin[sl], ssx[n2], g1bc[:, 0, :], ALU.mult, ALU.mult),
             reads=[B_xin[sl], B_ssx[n2], B_g1], writes=[B_xnb[n2]])

    def x_part2(tt):
        grp, sub = tt // 4, tt % 4
        sg = grp % 2
        n2 = tt % 8
        bank = 4 + tt % 2
        for kc in range(8):
            P.op("pe", TR(pbh[bank][:, kc * 128:(kc + 1) * 128], xnb[n2][:, kc * 128:(kc + 1) * 128], ident),
                 reads=[B_xnb[n2], B_const], writes=[PB[bank]])
        evac_copy(stg[sg][:, :, sub * 128:(sub + 1) * 128], pbh[bank][:, 0:1024].rearrange("p (k t) -> p k t", k=8),
                  [PB[bank]], [B_stg[sg]])
        if sub == 3:
            P.dma(DMA(xnT_v[:, :, grp * 512:(grp + 1) * 512], stg[sg]), ds_stg[sg], reads=[B_stg[sg]], writes=[XN[grp]])

    xq = {"next": 0, "pend": None}

    def x_step():
        if xq["pend"] is not None:
            x_part2(xq["pend"])
            xq["pend"] = None
        if xq["next"] < 64:
            x_part1(xq["next"])
            xq["pend"] = xq["next"]
            xq["next"] += 1

    mmrr = 0

    def r0_tile(i):
        nonlocal mmrr
        s2 = i % 2
        if i == 0:
            P.dma(DMA(xt[0], xnT_v[:, :, 0:512]), ds_xt[0], reads=[XN[0]], writes=[B_xt[0]])
        if i + 1 < 8:
            P.dma(DMA(xt[1 - s2], xnT_v[:, :, (i + 1) * 512:(i + 2) * 512]), ds_xt[1 - s2], reads=[XN[i + 1]], writes=[B_xt[1 - s2]])
        ng = 0
        for sub in range(4):
            for cg in range(6):
                bk = mmrr % 4
                mmrr += 1
                for kc in range(8):
                    P.op("pe", MM(pb[bk], xt[s2][:, kc, sub * 128:(sub + 1) * 128], wkv[:, kc, cg * 512:(cg + 1) * 512],
                                  kc == 0, kc == 7), reads=[B_xt[s2], B_wkv], writes=[PB[bk]])
                if cg < 2:
                    for hh in range(2):
                        h = cg * 2 + hh
                        P.op("dve", TS(Kz[s2][:, sub, h * 256:(h + 1) * 256], pb[bk][:, hh * 256:(hh + 1) * 256],
                                       zeta512[:, sub, h:h + 1], None, ALU.mult),
                             reads=[PB[bk], B_const], writes=[B_Kz[s2]])
                else:
                    evac_copy(Vr[s2][:, sub, (cg - 2) * 512:(cg - 1) * 512], pb[bk], [PB[bk]], [B_Vr[s2]])
                ng += 1
        for hc in range(8):
            h = hc // 2
            bk = 6 + hc % 2
            for sub in range(4):
                P.op("pe", MM(pb[bk], Kz[s2][:, sub, hc * 128:(hc + 1) * 128], Vr[s2][:, sub, h * 512:(h + 1) * 512],
                              sub == 0, sub == 3), reads=[B_Kz[s2], B_Vr[s2]], writes=[PB[bk]])
            P.op("dve", STT(S32[:, hc, :], S32[:, hc, :], _HC["g512"][h], pb[bk], ALU.mult, ALU.add),
                 reads=[PB[bk], B_S32[hc]], writes=[B_S32[hc]])

    def x_b1(b):
        for tt in range(8 * b, 8 * b + 8):
            x_part1(tt)

    def x_b2(b):
        for tt in range(8 * b, 8 * b + 8):
            x_part2(tt)

    x_b1(0)
    x_b2(0)
    x_b1(1)
    for i in range(8):
        r0_tile(i)
        if i + 1 < 8:
            x_b2(i + 1)
        if 1 <= i <= 4:
            watt_cast(i - 1)
        if i + 2 < 8:
            x_b1(i + 2)
    for hc in range(8):
        P.op("act", ACP(Sbf[:, hc, :], S32[:, hc, :]), reads=[B_S32[hc]], writes=[B_Sbf[hc]])
    P.barrier()
    if debug == "R0":
        dsd = P.dsem("dbg")
        P.dma(DMA(dbg_s.rearrange("p (a v) -> p a v", a=8), S32), dsd, reads=B_S32, writes=[Buf()])
        return finish()

    C = Carver(SBASE)
    W_A = C([3, 8, 3, 128], BF16)
    B_WA = Buf("W_A")
    ds_WA = P.dsem("W_A")
    abias = [C([3, 256], F32) for _ in range(2)]
    B_ab = [Buf("abias0"), Buf("abias1")]
    ds_ab = [P.dsem("abias0"), P.dsem("abias1")]
    xs = [C([8, 2048], BF16) for _ in range(2)]
    B_xs = [Buf() for _ in range(2)]
    ds_xs = [P.dsem("xs%d" % i) for i in range(2)]
    Kt = [[C([16, 128], BF16) for _ in range(2)] for _ in range(3)]
    Vt = [[C([16, 128], BF16) for _ in range(2)] for _ in range(3)]
    Qt = [C([16, 128], BF16) for _ in range(3)]
    B_Kt = [[Buf() for _ in range(2)] for _ in range(3)]
    B_Vt = [[Buf() for _ in range(2)] for _ in range(3)]
    B_Qt = [Buf() for _ in range(3)]
    sq_all = C([4, 512], BF16)
    sq = [sq_all[:, i, :] for i in range(4)]
    B_sq = [Buf() for _ in range(4)]
    oaT = sq_all.rearrange("p a b -> p (a b)")
    rin = [C([512], F32) for _ in range(2)]
    B_rin = [Buf() for _ in range(2)]
    sb32 = [C([256], F32) for _ in range(4)]
    B_sb = [Buf() for _ in range(4)]
    pT = [C([256], BF16) for _ in range(4)]
    B_pT = [Buf() for _ in range(4)]
    acc = C([2, 2048], F32)
    B_acc = Buf("acc")
    ds_oaT = P.dsem("oaT")
    OA = [[Buf() for _ in range(2)] for _ in range(4)]
    pairs = [(j, s) for j in range(4 if debug != "A1" else 1) for s in (1, 2, 3)]

    W_A_flat = W_A.rearrange("p a b c d -> p (a b c d)")

    def a_load_w(j):
        P.dma(DMA(W_A_flat, w_att_b[j]), ds_WA, reads=[WATT[j]], writes=[B_WA])
        P.dma(DMA(abias[j % 2], c_abias.rearrange("p (g j) c -> p g j c", g=3)[:, :, j, :]), ds_ab[j % 2], writes=[B_ab[j % 2]])

    def a_load_x(pi):
        s_ = pairs[pi][1]
        xl_ = pi % 2
        P.dma(DMA(xs[xl_], xnT_v[:, :, s_ * 2048:(s_ + 1) * 2048]), ds_xs[xl_],
              reads=[XN[4 * s_ + i] for i in range(4)], writes=[B_xs[xl_]])

    a_load_w(0)
    a_load_x(0)
    for pi, (j, s) in enumerate(pairs):
        halo = (s == 1)
        h = s % 2
        X = xs[pi % 2]
        B_X = B_xs[pi % 2]
        AB = abias[j % 2]
        B_AB = B_ab[j % 2]
        items = []
        vgroups = []
        for g in range(3):
            Dg = DIL[g]
            tss = [3] if (halo and g < 2) else [0, 1, 2, 3]
            for ts in tss:
                items.append((g, "k", ts))
                if not halo:
                    items.append((g, "q", ts))
            tiles = list(range(16 - Dg, 16)) if (halo and g < 2) else list(range(16))
            for b0 in range(0, len(tiles), 4):
                vgroups.append((g, tiles[b0:b0 + 4]))

        def stage0(ii):
            g, kind, ts = items[ii]
            bk = ii % 4
            i3 = 0 if kind == "q" else 1
            for kc in range(8):
                P.op("pe", MM(pb[bk], W_A[:, i3, kc, g, :], X[:, kc, ts * 512:(ts + 1) * 512], kc == 0, kc == 7),
                     reads=[B_WA, B_X], writes=[PB[bk]])
            P.op("act", ACTF(sq[bk], pb[bk], AF.Square), reads=[PB[bk]], writes=[B_sq[bk]])

        def stage1(ii):
            g, kind, ts = items[ii]
            Dg = DIL[g]
            hg = g * 4 + j
            bk = ii % 4
            qb = 4 + ii % 2
            r2 = ii % 2
            P.op("pe", MM(pb[qb], ones, sq[bk]), reads=[B_sq[bk], B_const], writes=[PB[qb]])
            P.op("act", ACTF(rin[r2], pb[qb], AF.Ln, bias=epsc, scale=1.0 / 128), reads=[PB[qb], B_const], writes=[B_rin[r2]])
            P.op("act", ACTF(rin[r2], rin[r2], AF.Exp, scale=-0.5), reads=[B_rin[r2]], writes=[B_rin[r2]])
            if kind == "q":
                dst_t, B_dst, gv = Qt[g], B_Qt[g], gq
            else:
                dst_t, B_dst, gv = Kt[g][h], B_Kt[g][h], gk
            if Dg == 1:
                dv = dst_t[:, ts * 4:(ts + 1) * 4, :]
                sv = pb[bk].rearrange("p (a l) -> p a l", a=4)
                rv = rin[r2].rearrange("p (a l) -> p a l", a=4)
            elif Dg == 4:
                dv = dst_t[:, ts * 4:(ts + 1) * 4, :]
                sv = pb[bk].rearrange("p (l r) -> p r l", r=4)
                rv = rin[r2].rearrange("p (l r) -> p r l", r=4)
            else:
                dv = dst_t[:, :, ts * 32:(ts + 1) * 32]
                sv = pb[bk].rearrange("p (l r) -> p r l", r=16)
                rv = rin[r2].rearrange("p (l r) -> p r l", r=16)
            P.op("dve", STT(dv, sv, gv[:, hg:hg + 1], rv, ALU.mult, ALU.mult),
                 reads=[PB[bk], B_rin[r2], B_const], writes=[B_dst])

        def vgroup(vi):
            g, grp_t = vgroups[vi]
            Dg = DIL[g]
            bk = 6 + vi % 2
            for ti, t in enumerate(grp_t):
                s_, r = t // Dg, t % Dg
                base = s_ * 128 * Dg + r
                for kc in range(8):
                    lv = X[:, kc, base:base + 127 * Dg + 1:Dg] if Dg > 1 else X[:, kc, base:base + 128]
                    P.op("pe", MM(pb[bk][:, ti * 128:(ti + 1) * 128], lv, W_A[:, 2, kc, g, :], kc == 0, kc == 7),
                         reads=[B_WA, B_X], writes=[PB[bk]])
            n = len(grp_t)
            evac_copy(Vt[g][h][:, grp_t[0]:grp_t[0] + n, :],
                      pb[bk][:, 0:n * 128].rearrange("p (a d) -> p a d", a=n), [PB[bk]], [B_Vt[g][h]])

        SK = 2
        vi = 0
        for step in range(len(items) + SK):
            if step < len(items):
                stage0(step)
            if vi < len(vgroups):
                vgroup(vi)
                vi += 1
            if step - SK >= 0:
                stage1(step - SK)
        while vi < len(vgroups):
            vgroup(vi)
            vi += 1
        if pi + 1 < len(pairs):
            if pairs[pi + 1][0] != j:
                a_load_w(pairs[pi + 1][0])
            a_load_x(pi + 1)
        issue_casts(2)
        if halo:
            continue
        units = [(g, t) for g in range(3) for t in range(16)]
        info = {}

        def front(ui):
            g, t = units[ui]
            Dg = DIL[g]
            bk = ui % 4
            if t - Dg >= 0:
                kp, vp, B_kp, B_vp, onp = Kt[g][h][:, t - Dg, :], Vt[g][h][:, t - Dg, :], B_Kt[g][h], B_Vt[g][h], ones
            else:
                tp = 16 + t - Dg
                kp, vp, B_kp, B_vp = Kt[g][1 - h][:, tp, :], Vt[g][1 - h][:, tp, :], B_Kt[g][1 - h], B_Vt[g][1 - h]
                onp = ones_h if s == 2 else ones
            kc_, vc_ = Kt[g][h][:, t, :], Vt[g][h][:, t, :]
            qv = Qt[g][:, t, :]
            P.op("pe", MM(pb[bk][:, 0:128], kp, qv), reads=[B_kp, B_Qt[g]], writes=[PB[bk]])
            P.op("pe", MM(pb[bk][:, 128:256], kc_, qv), reads=[B_Kt[g][h], B_Qt[g]], writes=[PB[bk]])
            P.op("dve", TT(sb32[bk], pb[bk][:, 0:256], AB[:, g, :], ALU.add), reads=[PB[bk], B_AB], writes=[B_sb[bk]])
            P.op("act", ACTF(pT[bk], sb32[bk], AF.Exp), reads=[B_sb[bk]], writes=[B_pT[bk]])
            info[ui] = (vp, vc_, onp, B_vp)

        def back(ui):
            g, t = units[ui]
            Dg = DIL[g]
            bk = ui % 4
            vp, vc_, onp, B_vp = info.pop(ui)
            P.op("pe", MM(pb[bk][:, 256:384], vp, pT[bk][:, 0:128], True, False), reads=[B_vp, B_pT[bk]], writes=[PB[bk]])
            P.op("pe", MM(pb[bk][:, 256:384], vc_, pT[bk][:, 128:256], False, True), reads=[B_Vt[g][h], B_pT[bk]], writes=[PB[bk]])
            P.op("pe", MM(pb[bk][:, 384:512], onp, pT[bk][:, 0:128], True, False), reads=[B_const, B_pT[bk]], writes=[PB[bk]])
            P.op("pe", MM(pb[bk][:, 384:512], ones, pT[bk][:, 128:256], False, True), reads=[B_const, B_pT[bk]], writes=[PB[bk]])
            s_, r = t // Dg, t % Dg
            base = s_ * 128 * Dg + r
            av = acc[:, :, base:base + 127 * Dg + 1:Dg] if Dg > 1 else acc[:, :, base:base + 128]
            pv = pb[bk][:, 256:512].rearrange("p (a q) -> p a q", a=2)
            if g == 0:
                P.op("dve", CP(av, pv), reads=[PB[bk]], writes=[B_acc])
            else:
                P.op("dve", TT(av, pv, av, ALU.add), reads=[PB[bk], B_acc], writes=[B_acc])

        for step in range(len(units) + SK):
            if step < len(units):
                front(step)
            if step - SK >= 0:
                back(step - SK)
        P.op("dve", RCP(acc[:, 1, :], acc[:, 1, :]), reads=[B_acc], writes=[B_acc])
        P.op("dve", TT(oaT, acc[:, 0, :], acc[:, 1, :], ALU.mult), reads=[B_acc], writes=B_sq)
        P.dma(DMA(oaT_d[j * 128:(j + 1) * 128, (s - 2) * 2048:(s - 1) * 2048], oaT), ds_oaT,
              reads=B_sq, writes=[OA[j][s - 2]])
    issue_casts(100)
    P.barrier()
    if debug in ("A", "A1"):
        return finish()

    C = Carver(SBASE)
    xtl = C([4, D], F32)
    B_xtl = Buf("xtl")
    ds_xtl = P.dsem("xtl")
    ds_x1 = P.dsem("x1st")
    xnt = [C([8, 512], BF16) for _ in range(2)]
    B_xnt = [Buf() for _ in range(2)]
    ds_xnt = [P.dsem("xnt%d" % i) for i in range(2)]
    oat = [C([4, 512], BF16) for _ in range(2)]
    B_oat = [Buf() for _ in range(2)]
    ds_oat = [P.dsem("oat%d" % i) for i in range(2)]
    NW = 4
    wbuf = [C([4096], BF16) for _ in range(NW)]
    B_wb = [Buf() for _ in range(NW)]
    ds_wb = [P.dsem("wb%d" % i) for i in range(NW)]
    qkT = C([16, 512], BF16)
    B_qT, B_kT = Buf("qT"), Buf("kT")
    KzB = C([4, 1024], BF16)
    B_KzB = Buf("KzB")
    VrB = C([4, 2048], BF16)
    B_VrB = Buf("VrB")
    gT = C([16, 512], BF16)
    B_gT = Buf("gT")
    oh = [C([512], BF16) for _ in range(4)]
    B_oh = [Buf() for _ in range(4)]
    PB7H = [Buf("pb7a"), Buf("pb7b")]
    orT = C([16, 512], BF16)
    B_orT = Buf("orT")
    pTr = [C([4, 128], BF16) for _ in range(2)]
    B_pTr = [Buf() for _ in range(2)]
    dpm = C([4, 128], F32)
    xibc = C([4, 512], F32)
    B_rc = Buf("retconst")
    ds_rc = newds()
    P.dma(DMA(dpm, c_dpm), ds_rc, writes=[B_rc])
    P.dma(DMA(xibc, c_xibc), ds_rc, writes=[B_rc])
    stats = [C([4, 6], F32) for _ in range(2)]
    mv = [C([4, 2], F32) for _ in range(2)]
    rstd = [C([4], F32) for _ in range(2)]
    nmr = [C([4], F32) for _ in range(2)]
    B_st = [[Buf() for _ in range(4)] for _ in range(2)]
    B_rs = [Buf() for _ in range(2)]
    B_nm = [Buf() for _ in range(2)]
    t32 = [C([4, 128], F32) for _ in range(2)]
    B_t32 = [Buf() for _ in range(2)]
    ya32 = [C([512], F32) for _ in range(2)]
    B_ya = [Buf() for _ in range(2)]
    yb32 = [C([512], F32) for _ in range(2)]
    B_yb = [Buf() for _ in range(2)]
    sgT = qkT
    yT = KzB.rearrange("p a b -> p (a b)").rearrange("p (a b) -> p a b", a=8)
    X1 = [Buf("x1_%d" % i) for i in range(8)]
    wa_v = wa_b.rearrange("(k p) c -> p k c", p=128)
    wb_v = wb_b.rearrange("(k p) c -> p k c", p=128)
    wout_v = wout_b.rearrange("(k p) c -> p k c", p=128)
    wcnt = 0
    bkrr = 0
    wav = C([4, 1024], BF16)
    B_wa = Buf("wa_s")
    P.dma(DMA(wav, wa_v), newds(), reads=[WB["wa"]], writes=[B_wa])

    wpre = {}

    def wload(src_v, KC, c0, ncols, wbuf_name, prefetch=False):
        nonlocal wcnt
        key = (wbuf_name, c0)
        if not prefetch and key in wpre:
            return wpre.pop(key)
        sl = wcnt % NW
        wcnt += 1
        view = wbuf[sl].rearrange("p (k c) -> p k c", k=KC)
        P.dma(DMA(view, src_v[:, :, c0:c0 + ncols]), ds_wb[sl], reads=[WB[wbuf_name if wbuf_name != "w_in" else "w_in_rest"]], writes=[B_wb[sl]])
        if prefetch:
            wpre[key] = (view, B_wb[sl])
        return view, B_wb[sl]

    def tile_loads(i_):
        tok0_ = i_ * 512
        s2_ = i_ % 2
        P.dma(DMA(xnt[s2_], xnT_v[:, :, HALF + tok0_:HALF + tok0_ + 512]), ds_xnt[s2_], reads=[XN[8 + i_]], writes=[B_xnt[s2_]])
        P.dma(DMA(oat[s2_], oaT_d.rearrange("(k p) t -> p k t", p=128)[:, :, tok0_:tok0_ + 512]), ds_oat[s2_],
              reads=[OA[jj][i_ // 4] for jj in range(4)], writes=[B_oat[s2_]])

    tile_loads(0)

    def nextbank():
        nonlocal bkrr
        b = bkrr % 3
        bkrr += 1
        return b

    import os
    KN = os.environ.get('KN', '')
    for i in range(8 if 'T1' not in KN else 1):
        tok0 = i * 512
        s2 = i % 2
        P.dma(DMA(xtl, x_main[tok0:tok0 + 512, :].rearrange("(s p) f -> p s f", p=128)), ds_xtl, writes=[B_xtl])
        XT, B_XT = xnt[s2], B_xnt[s2]

        def fm_group(c0, evac):
            wv, B_w = wload(w_in_v, 8, c0, 512, "w_in")
            for blk in range(4):
                bk = nextbank()
                for kc in range(8):
                    P.op("pe", MM(pb[bk], wv[:, kc, blk * 128:(blk + 1) * 128], XT[:, kc, :], kc == 0, kc == 7),
                         reads=[B_w, B_XT], writes=[PB[bk]])
                evac(blk, pb[bk], PB[bk])

        for grp in range(2):
            def ev_q(blk, ps, PBb, grp=grp):
                hc = grp * 4 + blk
                P.op("dve", TT(qkT[:, hc, :], ps, xibc[:, hc // 2, :], ALU.mult), reads=[PBb, B_rc], writes=[B_qT])
            fm_group(C_QR + grp * 512, ev_q)
        for grp in range(2):
            def ev_k(blk, ps, PBb, grp=grp):
                hc = grp * 4 + blk
                P.op("act", ACP(qkT[:, 8 + hc, :], ps), reads=[PBb], writes=[B_kT])
            fm_group(C_KR + grp * 512, ev_k)
        for sub in range(4):
            for hc in range(8):
                P.op("pe", TR(pbh[7][:, hc * 128:(hc + 1) * 128], qkT[:, 8 + hc, sub * 128:(sub + 1) * 128], ident),
                     reads=[B_kT, B_const], writes=[PB[7]])
            for h in range(4):
                P.op("dve", TS(KzB[:, sub, h * 256:(h + 1) * 256], pbh[7][:, h * 256:(h + 1) * 256],
                               zeta16[:, h:h + 1], None, ALU.mult), reads=[PB[7], B_const], writes=[B_KzB])
        for grp in range(4):
            wv, B_w = wload(w_in_v, 8, C_VR + grp * 512, 512, "w_in")
            for sub in range(4):
                bk = nextbank()
                for kc in range(8):
                    P.op("pe", MM(pb[bk], XT[:, kc, sub * 128:(sub + 1) * 128], wv[:, kc, :], kc == 0, kc == 7),
                         reads=[B_w, B_XT], writes=[PB[bk]])
                evac_copy(VrB[:, sub, grp * 512:(grp + 1) * 512], pb[bk], [PB[bk]], [B_VrB])
        for grp in range(4):
            def ev_g(blk, ps, PBb, grp=grp):
                P.op("act", ACTF(gT[:, grp * 4 + blk, :], ps, AF.Silu), reads=[PBb], writes=[B_gT])
            fm_group(C_GR + grp * 512, ev_g)
        def emit_A(sub):
            tsl_ = slice(sub * 128, (sub + 1) * 128)
            pr_ = sub % 2
            for h in range(4):
                for c in range(2):
                    hc = h * 2 + c
                    P.op("pe", MM(pb[0][:, h * 128:(h + 1) * 128], qkT[:, 8 + hc, tsl_], qkT[:, hc, tsl_], c == 0, c == 1),
                         reads=[B_kT, B_qT], writes=[PB[0]])
            P.op("dve", TT(pTr[pr_], pb[0].rearrange("p (h q) -> p h q", h=4), dpm, ALU.mult),
                 reads=[PB[0], B_rc], writes=[B_pTr[pr_]])

        emit_A(0)
        for sub in range(4):
            tsl = slice(sub * 128, (sub + 1) * 128)
            pr = sub % 2
            if sub > 0 and 'NOA' in KN:
                emit_A(sub)
            for h in range(4):
                ob = 1 + h
                P.op("pe", MM(pb[ob], pTr[pr][:, h, :], VrB[:, sub, h * 512:(h + 1) * 512], True, False),
                     reads=[B_pTr[pr], B_VrB], writes=[PB[ob]])
                for c in range(2):
                    hc = h * 2 + c
                    P.op("pe", MM(pb[ob], qkT[:, hc, tsl], Sbf[:, hc, :], False, c == 1),
                         reads=[B_qT, B_Sbf[hc]], writes=[PB[ob]])
            for h in range(4):
                ob = 1 + h
                P.op("dve", lambda e, o_=stats[pr][:, h, :], i_=pb[ob]: e.bn_stats(out=o_, in_=i_),
                     reads=[PB[ob]], writes=[B_st[pr][h]])
                P.op("dve", lambda e, o_=mv[pr][:, h, :], i_=stats[pr][:, h, :]: e.bn_aggr(out=o_, in_=i_),
                     reads=[B_st[pr][h]], writes=[B_st[pr][h]])
            P.op("act", ACTF(rstd[pr], mv[pr][:, :, 1], AF.Ln, bias=epsc, scale=1.0),
                 reads=B_st[pr] + [B_const], writes=[B_rs[pr]])
            P.op("act", ACTF(rstd[pr], rstd[pr], AF.Exp, scale=-0.5), reads=[B_rs[pr]], writes=[B_rs[pr]])
            P.op("dve", STT(nmr[pr], mv[pr][:, :, 0], -1.0, rstd[pr], ALU.mult, ALU.mult),
                 reads=B_st[pr] + [B_rs[pr]], writes=[B_nm[pr]])
            for h in range(4):
                ob = 1 + h
                P.op("act", ACTF(oh[h], pb[ob], AF.Identity, bias=nmr[pr][:, h:h + 1], scale=rstd[pr][:, h:h + 1]),
                     reads=[PB[ob], B_rs[pr], B_nm[pr]], writes=[B_oh[h]])
            for hc in range(8):
                h = hc // 2
                sbk = 5 if hc % 2 else 0
                P.op("pe", MM(pb[sbk], KzB[:, sub, hc * 128:(hc + 1) * 128], VrB[:, sub, h * 512:(h + 1) * 512]),
                     reads=[B_KzB, B_VrB], writes=[PB[sbk]])
                P.op("dve", STT(S32[:, hc, :], S32[:, hc, :], _HC["g128"][h], pb[sbk], ALU.mult, ALU.add),
                     reads=[PB[sbk], B_S32[hc]], writes=[B_S32[hc]])
                if hc % 2 == 0:
                    P.op("act", ACP(Sbf[:, hc, :], S32[:, hc, :]), reads=[B_S32[hc]], writes=[B_Sbf[hc]])
                else:
                    P.op("pool", CP(Sbf[:, hc, :], S32[:, hc, :]), reads=[B_S32[hc]], writes=[B_Sbf[hc]])
            if sub < 3 and 'NOA' not in KN:
                emit_A(sub + 1)
            for h in range(4):
                tb_ = 6 + h % 2
                for fc in range(4):
                    P.op("pe", TR(pbh[tb_][:, fc * 128:(fc + 1) * 128], oh[h][:, fc * 128:(fc + 1) * 128], ident),
                         reads=[B_oh[h], B_const], writes=[PB[tb_]])
                for fc in range(4):
                    f = h * 4 + fc
                    if fc % 2 == 0:
                        P.op("act", ACTF(t32[h % 2][:, fc, :], pbh[tb_][:, fc * 128:(fc + 1) * 128], AF.Identity,
                                         bias=gnb[:, f:f + 1], scale=gng[:, f:f + 1]),
                             reads=[PB[tb_], B_const], writes=[B_t32[h % 2]])
                    else:
                        P.op("dve", TS(t32[h % 2][:, fc, :], pbh[tb_][:, fc * 128:(fc + 1) * 128],
                                       gng[:, f:f + 1], gnb[:, f:f + 1], ALU.mult, ALU.add),
                             reads=[PB[tb_], B_const], writes=[B_t32[h % 2]])
                P.op("pool", TT(orT[:, h * 4:(h + 1) * 4, tsl], t32[h % 2], gT[:, h * 4:(h + 1) * 4, tsl], ALU.mult),
                     reads=[B_t32[h % 2], B_gT], writes=[B_orT])
        for grp in range(4):
            def ev_s(blk, ps, PBb, grp=grp):
                P.op("act", ACTF(sgT[:, grp * 4 + blk, :], ps, AF.Sigmoid), reads=[PBb], writes=[B_qT, B_kT])
            fm_group(C_GA + grp * 512, ev_s)
        for og in range(4):
            wbv, B_wbv = wload(wb_v, 16, og * 256, 256, "wb")
            for o2 in range(2):
                oc = og * 2 + o2
                y2 = oc % 2
                bka = nextbank()
                for kc in range(4):
                    P.op("pe", MM(pb[bka], wav[:, kc, oc * 128:(oc + 1) * 128], oat[s2][:, kc, :], kc == 0, kc == 3),
                         reads=[B_wa, B_oat[s2]], writes=[PB[bka]])
                P.op("dve", TT(ya32[y2], pb[bka], sgT[:, oc, :], ALU.mult), reads=[PB[bka], B_qT, B_kT], writes=[B_ya[y2]])
                bkb = nextbank()
                for kc in range(16):
                    P.op("pe", MM(pb[bkb], wbv[:, kc, o2 * 128:(o2 + 1) * 128], orT[:, kc, :], kc == 0, kc == 15),
                         reads=[B_wbv, B_orT], writes=[PB[bkb]])
                P.op("dve", TT(yb32[y2], pb[bkb], sgT[:, 8 + oc, :], ALU.mult), reads=[PB[bkb], B_qT, B_kT], writes=[B_yb[y2]])
                P.op("pool", TT(yT[:, oc, :], ya32[y2], yb32[y2], ALU.add), reads=[B_ya[y2], B_yb[y2]], writes=[B_KzB])
        if dbg and i == 0:
            dsd = P.dsem("dbg")
            P.dma(DMA(dbg_or.rearrange("p (a v) -> p a v", a=16), orT), dsd, reads=[B_orT], writes=[Buf()])
            P.dma(DMA(dbg_y.rearrange("p (a v) -> p a v", a=8), yT), dsd, reads=[B_KzB], writes=[Buf()])
            P.dma(DMA(dbg_sg.rearrange("p (a v) -> p a v", a=16), sgT), dsd, reads=[B_qT, B_kT], writes=[Buf()])
        for grp in range(2):
            wv, B_w = wload(wout_v, 8, grp * 512, 512, "wout")
            for sub in range(4):
                bk = nextbank()
                for kc in range(8):
                    P.op("pe", MM(pb[bk], yT[:, kc, sub * 128:(sub + 1) * 128], wv[:, kc, :], kc == 0, kc == 7),
                         reads=[B_w, B_KzB], writes=[PB[bk]])
                P.op("dve", TT(xtl[:, sub, grp * 512:(grp + 1) * 512], pb[bk], xtl[:, sub, grp * 512:(grp + 1) * 512], ALU.add),
                     reads=[PB[bk], B_xtl], writes=[B_xtl])
        if i + 1 < 8 and 'T1' not in KN:
            tile_loads(i + 1)
            wload(w_in_v, 8, C_QR, 512, "w_in", prefetch=True)
            wload(w_in_v, 8, C_QR + 512, 512, "w_in", prefetch=True)
        P.dma(DMA(out[tok0:tok0 + 512, :].rearrange("(s p) f -> p s f", p=128), xtl), ds_x1, reads=[B_xtl], writes=[X1[i]])
    P.barrier()
    if debug == "B1":
        return finish()

    C = Carver(GBASE)
    wup_s = C([8, 4096], BF16)
    wdn_s = C([32, 1024], BF16)
    B_wup, B_wdn = Buf("wup"), Buf("wdn")
    ds_w2 = P.dsem("w2")
    ds_w2b = P.dsem("w2b")
    wup_v = wup_b.rearrange("(k p) c -> p k c", p=128)
    wdn_v = wdown_b.rearrange("(k p) c -> p k c", p=128)
    B_wupq = [Buf("wup%d" % q) for q in range(4)]
    ds_wupq = [P.dsem("wupq%d" % q) for q in range(4)]
    for q4 in range(4):
        P.dma(DMA(wup_s[:, :, q4 * 1024:(q4 + 1) * 1024], wup_v[:, :, q4 * 1024:(q4 + 1) * 1024]), ds_wupq[q4],
              reads=[WB["wup"]], writes=[B_wupq[q4]])
    for q4 in range(4):
        P.dma(DMA(wdn_s[:, q4 * 8:(q4 + 1) * 8, :], wdn_v[:, q4 * 8:(q4 + 1) * 8, :]), ds_w2b,
              reads=[WB["wdown"]], writes=[B_wdn])
    g2bc = C([1, D], F32)
    B_g2 = Buf("g2bc")
    P.dma(DMA(g2bc, norm2_g.partition_broadcast(128)), newds(), writes=[B_g2])
    x1t = [C([2, D], F32) for _ in range(3)]
    B_x1t = [Buf() for _ in range(3)]
    ds_x1t = [P.dsem("x1t%d" % i) for i in range(3)]
    ds_ot = [P.dsem("ot%d" % i) for i in range(3)]
    junk2 = C([D], BF16)
    B_junk2 = Buf()
    ss2 = [C([1], F32) for _ in range(2)]
    B_ss2 = [Buf() for _ in range(2)]
    xnb2 = [C([D], BF16) for _ in range(2)]
    B_xnb2 = [Buf() for _ in range(2)]
    xn2T = [C([8, 256], BF16) for _ in range(2)]
    B_xn2T = [Buf() for _ in range(2)]
    hT = [C([32, 256], BF16) for _ in range(2)]
    B_hT = [Buf("hT0"), Buf("hT1")]
    rl = [C([256], BF16) for _ in range(2)]
    B_rl = [Buf() for _ in range(2)]
    OUTB = [Buf() for _ in range(16)]

    def b2_norm1(i):
        tok0 = i * 256
        s3 = i % 3
        P.dma(DMA(x1t[s3], out[tok0:tok0 + 256, :].rearrange("(s p) f -> p s f", p=128)), ds_x1t[s3],
              reads=[X1[i // 2]], writes=[B_x1t[s3]])
        for sub in range(2):
            xi_ = x1t[s3][:, sub, :]
            P.op("act", ACTF(junk2, xi_, AF.Square, accum_out=ss2[sub]), reads=[B_x1t[s3]], writes=[B_junk2, B_ss2[sub]])
            P.op("act", ACTF(ss2[sub], ss2[sub], AF.Ln, bias=epsc, scale=1.0 / D), reads=[B_ss2[sub], B_const], writes=[B_ss2[sub]])
            P.op("act", ACTF(ss2[sub], ss2[sub], AF.Exp, scale=-0.5), reads=[B_ss2[sub]], writes=[B_ss2[sub]])
            P.op("dve", STT(xnb2[sub], xi_, ss2[sub], g2bc[:, 0, :], ALU.mult, ALU.mult),
                 reads=[B_x1t[s3], B_ss2[sub], B_g2], writes=[B_xnb2[sub]])

    def b2_norm2(i):
        s2 = i % 2
        for sub in range(2):
            for kc in range(8):
                P.op("pe", TR(pbh[sub][:, kc * 128:(kc + 1) * 128], xnb2[sub][:, kc * 128:(kc + 1) * 128], ident),
                     reads=[B_xnb2[sub], B_const], writes=[PB[sub]])
            evac_copy(xn2T[s2][:, :, sub * 128:(sub + 1) * 128], pbh[sub][:, 0:1024].rearrange("p (k t) -> p k t", k=8),
                      [PB[sub]], [B_xn2T[s2]])

    def b2_up(i):
        s2 = i % 2
        for fc in range(32):
            bk = 2 + fc % 3
            for kc in range(8):
                P.op("pe", MM(pb[bk][:, 0:256], wup_s[:, kc, fc * 128:(fc + 1) * 128], xn2T[s2][:, kc, :], kc == 0, kc == 7),
                     reads=[B_wupq[fc // 8], B_xn2T[s2]], writes=[PB[bk]])
            r2 = fc % 2
            P.op("act", ACTF(rl[r2], pb[bk][:, 0:256], AF.Relu), reads=[PB[bk]], writes=[B_rl[r2]])
            P.op("pool", TT(hT[s2][:, fc, :], rl[r2], rl[r2], ALU.mult), reads=[B_rl[r2]], writes=[B_hT[s2]])

    def b2_down(i):
        tok0 = i * 256
        s2 = i % 2
        s3 = i % 3
        for sub in range(2):
            for grp in range(2):
                bk = 5 + (sub * 2 + grp) % 3
                for kc in range(32):
                    P.op("pe", MM(pb[bk], hT[s2][:, kc, sub * 128:(sub + 1) * 128], wdn_s[:, kc, grp * 512:(grp + 1) * 512],
                                  kc == 0, kc == 31), reads=[B_hT[s2], B_wdn], writes=[PB[bk]])
                P.op("dve", TT(x1t[s3][:, sub, grp * 512:(grp + 1) * 512], pb[bk], x1t[s3][:, sub, grp * 512:(grp + 1) * 512],
                               ALU.add), reads=[PB[bk], B_x1t[s3]], writes=[B_x1t[s3]])
        P.dma(DMA(out[tok0:tok0 + 256, :].rearrange("(s p) f -> p s f", p=128), x1t[s3]), ds_ot[s3],
              reads=[B_x1t[s3]], writes=[OUTB[i]])

    b2_norm1(0)
    b2_norm2(0)
    for i in range(16):
        b2_up(i)
        if i + 1 < 16:
            b2_norm1(i + 1)
        if i >= 1:
            b2_down(i - 1)
        if i + 1 < 16:
            b2_norm2(i + 1)
    b2_down(15)
    P.wait_bufs("sp", OUTB)
    return finish()


def _in_maps(inputs):
    x = np.ascontiguousarray(inputs["x"], dtype=np.float32)
    shared = {
        "norm1_g": inputs["norm1_g"].reshape(1, D), "norm2_g": inputs["norm2_g"].reshape(1, D),
        "w_in": inputs["w_in"].reshape(D, IN_W), "q_norm_g": inputs["q_norm_g"].reshape(12, 128),
        "k_norm_g": inputs["k_norm_g"].reshape(12, 128), "ret_gn_g": inputs["ret_gn_g"].reshape(1, 2048),
        "ret_gn_b": inputs["ret_gn_b"].reshape(1, 2048), "w_proj_a": inputs["w_proj_a"].reshape(512, D),
        "w_proj_b": inputs["w_proj_b"].reshape(2048, D), "w_out": inputs["w_out"].reshape(D, D),
        "w_up": inputs["w_up"].reshape(D, 4096), "w_down": inputs["w_down"].reshape(4096, D),
        "c_ident": _HC["ident"], "c_abias": _HC["abias"], "c_dpm": _HC["dpm"], "c_xibc": _HC["xibc"],
        "c_zeta16": _HC["zeta16"], "c_zeta512": _HC["zeta512"],
    }
    shared = {k: np.ascontiguousarray(v, dtype=np.float32) for k, v in shared.items()}
    maps = []
    zeros = np.zeros((HALF, D), np.float32)
    for c in range(8):
        b, half = c // 2, c % 2
        m = dict(shared)
        m["x_main"] = x[b, half * HALF:(half + 1) * HALF]
        m["x_prev"] = x[b, 0:HALF] if half else zeros
        m["hflag"] = np.full((128, 128), float(half), np.float32)
        maps.append(m)
    return maps


def kernel(**inputs):
    nc = build_nc()
    maps = _in_maps(inputs)
    res = run_bass_kernel_spmd(nc, maps, core_ids=list(range(8)))
    outp = np.empty((4, 2 * HALF, D), np.float32)
    for c in range(8):
        b, half = c // 2, c % 2
        outp[b, half * HALF:(half + 1) * HALF] = res.results[c]["out"]
    return outp
```

```python
import contextlib
import numpy as np
import concourse.bass as bass
import concourse.mybir as mybir
from concourse.bass_utils import run_bass_kernel_spmd

F32 = mybir.dt.float32
BF16 = mybir.dt.bfloat16
AF = mybir.ActivationFunctionType
ALU = mybir.AluOpType

COMPUTE = ("pe", "act", "dve", "pool")
QUEUES = ("sp",)
EPS = 1e-6
NEG = -30000.0


def MM(out, lhsT, rhs, start=True, stop=True):
    return lambda e: e.matmul(out, lhsT=lhsT, rhs=rhs, start=start, stop=stop)


def TR(out, in_, identity):
    return lambda e: e.transpose(out=out, in_=in_, identity=identity)


def ACTF(out, in_, func, **kw):
    return lambda e: e.activation(out=out, in_=in_, func=func, **kw)


def ACP(out, in_):
    return lambda e: e.copy(out=out, in_=in_)


def CP(out, in_):
    return lambda e: e.tensor_copy(out=out, in_=in_)


def TT(out, in0, in1, op):
    return lambda e: e.tensor_tensor(out=out, in0=in0, in1=in1, op=op)


def TS(out, in0, s1, s2, op0, op1=None):
    if op1 is None:
        return lambda e: e.tensor_scalar(out=out, in0=in0, scalar1=s1, scalar2=None, op0=op0)
    return lambda e: e.tensor_scalar(out=out, in0=in0, scalar1=s1, scalar2=s2, op0=op0, op1=op1)


def STT(out, in0, scalar, in1, op0, op1):
    return lambda e: e.scalar_tensor_tensor(out=out, in0=in0, scalar=scalar, in1=in1, op0=op0, op1=op1)


def MS(ap, val):
    return lambda e: e.memset(ap, val)


def RCP(out, in_):
    return lambda e: e.reciprocal(out=out, in_=in_)


def DMA(out, in_, **kw):
    return lambda e: e.dma_start(out=out, in_=in_, **kw)


class Buf:
    __slots__ = ("name", "lws", "reads")

    def __init__(self, name=""):
        self.name = name
        self.lws = {}
        self.reads = []


class DmaSem:
    __slots__ = ("sem", "val", "name")

    def __init__(self, name):
        self.name = name
        self.sem = None
        self.val = 0


class Prog:
    def __init__(self, nc):
        self.nc = nc
        self.ops = {e: [] for e in COMPUTE + QUEUES}
        self.seen = {e: {} for e in COMPUTE + QUEUES}
        self.dsems = []
        self.stack = contextlib.ExitStack()

    def dsem(self, name):
        d = DmaSem(name)
        self.dsems.append(d)
        return d

    def _need(self, eng, dep, waits, kind):
        if dep[0] == "eng":
            _, e2, idx = dep
            if e2 == eng and (eng == "pe" or kind != "raw"):
                return
            key = ("eng", e2)
            if self.seen[eng].get(key, -1) >= idx:
                return
            self.seen[eng][key] = idx
            self.ops[e2][idx][2] = True
            waits.append(dep)
        else:
            _, ds, val = dep
            key = ("dma", id(ds))
            if self.seen[eng].get(key, -1) >= val:
                return
            self.seen[eng][key] = val
            waits.append(dep)

    def _deps(self, eng, reads, writes):
        waits = []
        for b in reads:
            for d in b.lws.values():
                self._need(eng, d, waits, "raw")
        for b in writes:
            for d in b.lws.values():
                self._need(eng, d, waits, "waw")
            for r in b.reads:
                self._need(eng, r, waits, "war")
        return waits

    def _commit(self, me, key, reads, writes):
        for b in reads:
            b.reads.append(me)
        for b in writes:
            b.lws[key] = me
            b.reads = []

    def op(self, eng, fn, reads=(), writes=()):
        waits = self._deps(eng, reads, writes)
        idx = len(self.ops[eng])
        self.ops[eng].append([waits, fn, False, None])
        self._commit(("eng", eng, idx), eng, reads, writes)

    def dma(self, fn, ds, reads=(), writes=(), queue="sp"):
        waits = self._deps(queue, reads, writes)
        ds.val += 16
        self.ops[queue].append([waits, fn, False, ds])
        self._commit(("dma", ds, ds.val), ("dma", id(ds)), reads, writes)

    def barrier(self):
        lasts = {}
        for e in COMPUTE:
            for i in range(len(self.ops[e]) - 1, -1, -1):
                o = self.ops[e][i]
                if o[1] is not None and o[3] is None:
                    lasts[e] = i
                    break
        dvals = [(d, d.val) for d in self.dsems if d.val > 0]
        for eng in COMPUTE + QUEUES:
            waits = []
            for e2, idx in lasts.items():
                if e2 == eng and eng == "pe":
                    continue
                self._need(eng, ("eng", e2, idx), waits, "raw")
            for d, v in dvals:
                self._need(eng, ("dma", d, v), waits, "raw")
            self.ops[eng].append([waits, None, False, None])

    def wait_bufs(self, eng, bufs):
        waits = []
        for b in bufs:
            for d in b.lws.values():
                self._need(eng, d, waits, "raw")
        self.ops[eng].append([waits, None, False, None])

    def flush(self):
        nc = self.nc
        st = self.stack
        esem = {e: st.enter_context(nc.semaphore("s_" + e)) for e in COMPUTE}
        for d in self.dsems:
            d.sem = st.enter_context(nc.semaphore("d_" + d.name))
        sig = {}
        for e in COMPUTE:
            c = 0
            arr = []
            for o in self.ops[e]:
                if o[2]:
                    c += 1
                arr.append(c)
            sig[e] = arr
        block = st.enter_context(nc.Block())

        def emit(ename):
            def body(eng):
                for waits, fn, signaling, ds in self.ops[ename]:
                    for w in waits:
                        if w[0] == "eng":
                            eng.wait_ge(esem[w[1]], sig[w[1]][w[2]])
                        else:
                            eng.wait_ge(w[1].sem, w[2])
                    if fn is None:
                        continue
                    ins = fn(eng)
                    if ds is not None:
                        ins.then_inc(ds.sem, 16)
                    elif signaling:
                        ins.then_inc(esem[ename], 1)
            return body

        block.tensor(emit("pe"))
        block.scalar(emit("act"))
        block.vector(emit("dve"))
        block.gpsimd(emit("pool"))
        block.sync(emit("sp"))
        st.close()


D = 1024
HALF = 4096
IN_W = 12800
C_QA, C_KA, C_VA, C_QR, C_KR, C_VR, C_GR, C_GA, C_GB = 0, 1536, 3072, 4608, 5632, 6656, 8704, 10752, 11776
DIL = (1, 4, 16)
ARENA = 212480


def _host_consts():
    c = {}
    c["ident"] = np.eye(128, dtype=np.float32)
    slopes = 2.0 ** (-8.0 * np.arange(1, 13, dtype=np.float32) / 12)
    kj = np.arange(128)[:, None]
    qi = np.arange(128)[None, :]
    ab = np.empty((128, 12, 256), np.float32)
    for h in range(12):
        dil = DIL[h // 4]
        dprev = (128 + qi - kj).astype(np.float32)
        dcur = (qi - kj).astype(np.float32)
        ab[:, h, 0:128] = np.where(kj >= qi, -slopes[h] * dprev * dil, NEG)
        ab[:, h, 128:256] = np.where(kj <= qi, -slopes[h] * dcur * dil, NEG)
    c["abias"] = ab
    lg = np.log(1.0 - 2.0 ** (-5.0 - np.arange(4, dtype=np.float64)))
    p = np.arange(128, dtype=np.float64)
    dp = np.zeros((128, 4, 128), np.float64)
    for h in range(4):
        dp[:, h, :] = np.where(kj <= qi, np.exp(-lg[h] * (p[:, None] + 1.0)) / 16.0, 0.0)
    c["dpm"] = dp.astype(np.float32)
    xi = np.exp(lg[:, None] * (p[None, :] + 1.0))
    c["xibc"] = np.broadcast_to(np.tile(xi, (1, 4))[None], (128, 4, 512)).astype(np.float32).copy()
    c["zeta16"] = (np.exp(lg[None, :] * (127.0 - p[:, None])) / 16.0).astype(np.float32)
    z5 = np.empty((128, 4, 4), np.float64)
    for sub in range(4):
        z5[:, sub, :] = np.exp(lg[None, :] * (511.0 - (sub * 128 + p[:, None]))) / 16.0
    c["zeta512"] = z5.astype(np.float32)
    c["g128"] = [float(np.exp(lg[h] * 128.0)) for h in range(4)]
    c["g512"] = [float(np.exp(lg[h] * 512.0)) for h in range(4)]
    return c


_HC = _host_consts()


def build_nc(debug=False):
    nc = bass.Bass("TRN2", target_bir_lowering=False)

    def din(name, shape):
        return nc.dram_tensor(name, list(shape), F32, kind="ExternalInput").ap()

    def dscr(name, shape, dt, out=False):
        return nc.dram_tensor(name, list(shape), dt, kind="ExternalOutput" if out else "Internal").ap()

    x_main = din("x_main", [HALF, D])
    x_prev = din("x_prev", [HALF, D])
    hflag = din("hflag", [128, 128])
    norm1_g = din("norm1_g", [1, D])
    norm2_g = din("norm2_g", [1, D])
    w_in = din("w_in", [D, IN_W])
    q_norm_g = din("q_norm_g", [12, 128])
    k_norm_g = din("k_norm_g", [12, 128])
    ret_gn_g = din("ret_gn_g", [1, 2048])
    ret_gn_b = din("ret_gn_b", [1, 2048])
    w_proj_a = din("w_proj_a", [512, D])
    w_proj_b = din("w_proj_b", [2048, D])
    w_out = din("w_out", [D, D])
    w_up = din("w_up", [D, 4096])
    w_down = din("w_down", [4096, D])
    c_ident = din("c_ident", [128, 128])
    c_abias = din("c_abias", [128, 12, 256])
    c_dpm = din("c_dpm", [128, 4, 128])
    c_xibc = din("c_xibc", [128, 4, 512])
    c_zeta16 = din("c_zeta16", [128, 4])
    c_zeta512 = din("c_zeta512", [128, 4, 4])
    out = nc.dram_tensor("out", [HALF, D], F32, kind="ExternalOutput").ap()

    w_in_b = dscr("w_in_b", [D, IN_W], BF16)
    wa_b = dscr("wa_b", [512, D], BF16)
    wb_b = dscr("wb_b", [2048, D], BF16)
    wout_b = dscr("wout_b", [D, D], BF16)
    wup_b = dscr("wup_b", [D, 4096], BF16)
    wdown_b = dscr("wdown_b", [4096, D], BF16)
    wb_t = dscr("wb_t", [4, 128, 4096], BF16)
    w_att_b = dscr("w_att_b", [4, 128, 9216], BF16)
    dbg = bool(debug)
    xnT_d = dscr("xnT_d", [D, 2 * HALF], BF16, out=dbg)
    oaT_d = dscr("oaT_d", [512, HALF], BF16, out=dbg)
    dbg_s = dscr("dbg_s", [128, 8 * 512], F32, out=True) if dbg else None
    dbg_or = dscr("dbg_or", [128, 16 * 512], BF16, out=True) if dbg else None
    dbg_y = dscr("dbg_y", [128, 8 * 512], BF16, out=True) if dbg else None
    dbg_sg = dscr("dbg_sg", [128, 16 * 512], BF16, out=True) if dbg else None

    P = Prog(nc)
    st = P.stack
    arena = st.enter_context(nc.sbuf_tensor("arena", [128, ARENA // 2], BF16))
    pbt = [st.enter_context(nc.psum_tensor("pb%d" % i, [128, 512], F32)) for i in range(8)]
    pb = [t[:] for t in pbt]
    pbh = [t[:].bitcast(BF16) for t in pbt]
    PB = [Buf("pb%d" % i) for i in range(8)]
    letters = "abcdefg"

    class Carver:
        def __init__(self, base):
            self.off = base

        def __call__(self, shape, dt):
            n = int(np.prod(shape))
            esz = 4 if dt == F32 else 2
            nb = (n * esz + 31) // 32 * 32
            a = arena[:, self.off // 2: self.off // 2 + n * esz // 2]
            self.off += nb
            assert self.off <= ARENA, ("arena overflow", self.off)
            if dt == F32:
                a = a.bitcast(F32)
            if len(shape) > 1:
                names = " ".join(letters[i] for i in range(len(shape)))
                kw = {letters[i]: int(shape[i]) for i in range(len(shape))}
                a = a.rearrange("p (%s) -> p %s" % (names, names), **kw)
            return a

    def finish():
        P.barrier()
        P.flush()
        return nc

    G = Carver(0)
    ident = G([128], BF16)
    ones = G([128], BF16)
    ones_h = G([128], BF16)
    epsc = G([1], F32)
    mhalf = G([4], F32)
    gq = G([12], F32)
    gk = G([12], F32)
    gng = G([16], F32)
    gnb = G([16], F32)
    zeta16 = G([4], F32)
    zeta512 = G([4, 4], F32)
    tmpc = G([128], F32)
    tmpc2 = G([128], F32)
    B_const = Buf("const")
    B_tmpc = Buf("tmpc")
    B_tmpc2 = Buf("tmpc2")
    ds_c = P.dsem("const")
    dsn = [0]

    def newds():
        dsn[0] += 1
        return P.dsem("c%d" % dsn[0])
    GBASE = (G.off + 63) // 64 * 64

    P.dma(DMA(tmpc, c_ident), newds(), writes=[B_tmpc])
    P.dma(DMA(tmpc2, hflag), newds(), writes=[B_tmpc2])
    P.dma(DMA(gq, q_norm_g.rearrange("h d -> d h"), allow_slow_non_contiguous=True), ds_c, writes=[B_const])
    P.dma(DMA(gk, k_norm_g.rearrange("h d -> d h"), allow_slow_non_contiguous=True), ds_c, writes=[B_const])
    P.dma(DMA(gng, ret_gn_g.rearrange("o (c p) -> p (o c)", p=128), allow_slow_non_contiguous=True), ds_c, writes=[B_const])
    P.dma(DMA(gnb, ret_gn_b.rearrange("o (c p) -> p (o c)", p=128), allow_slow_non_contiguous=True), ds_c, writes=[B_const])
    P.dma(DMA(zeta16, c_zeta16), ds_c, writes=[B_const])
    P.dma(DMA(zeta512, c_zeta512), ds_c, writes=[B_const])
    P.op("dve", CP(ident, tmpc), reads=[B_tmpc], writes=[B_const])
    P.op("dve", CP(ones_h, tmpc2), reads=[B_tmpc2], writes=[B_const])
    P.op("dve", MS(ones, 1.0), writes=[B_const])
    P.op("dve", MS(epsc, EPS), writes=[B_const])
    P.op("dve", MS(mhalf, -0.5), writes=[B_const])
    P.op("dve", TS(gq, gq, float(128.0 ** -0.5), None, ALU.mult), reads=[B_const], writes=[B_const])

    C = Carver(GBASE)
    S32 = C([8, 512], F32)
    Sbf = C([8, 512], BF16)
    B_S32 = [Buf("S32_%d" % i) for i in range(8)]
    B_Sbf = [Buf("Sbf_%d" % i) for i in range(8)]
    SBASE = (C.off + 63) // 64 * 64
    C = Carver(SBASE)
    wkv = C([8, 3072], BF16)
    B_wkv = Buf("wkv")
    ds_wkv = P.dsem("wkv")
    w_in_f = w_in.rearrange("(k p) c -> p k c", p=128)
    for i in range(6):
        P.dma(DMA(wkv[:, :, i * 512:(i + 1) * 512], w_in_f[:, :, C_KR + i * 512:C_KR + (i + 1) * 512]),
              ds_wkv, writes=[B_wkv], queue="pool")
    WB = {}
    cast_jobs = []
    for name, src, dst, rows, nch, c0, c1 in (
        ("w_in_rest", w_in, w_in_b, D, 8, C_QR, IN_W),
        ("wa", w_proj_a, wa_b, 512, 1, 0, D),
        ("wout", w_out, wout_b, D, 1, 0, D), ("wup", w_up, wup_b, D, 4, 0, 4096), ("wdown", w_down, wdown_b, 4096, 4, 0, D),
    ):
        b = Buf(name)
        ds = P.dsem("w_" + name)
        step = rows // nch
        for i in range(nch):
            cast_jobs.append((dst[i * step:(i + 1) * step, c0:c1], src[i * step:(i + 1) * step, c0:c1], ds, b))
        WB[name] = b

    b_wbt = Buf("wb")
    ds_wbt = P.dsem("w_wb")
    for og_ in range(4):
        cast_jobs.append([(wb_t[og_][:, kc_ * 256:(kc_ + 1) * 256],
                           w_proj_b[kc_ * 128:(kc_ + 1) * 128, og_ * 256:(og_ + 1) * 256], ds_wbt, b_wbt) for kc_ in range(16)])
    WB["wb"] = b_wbt

    def issue_casts(n):
        for _ in range(n):
            if cast_jobs:
                job = cast_jobs.pop(0)
                if not isinstance(job, list):
                    job = [job]
                for dst_, src_, ds_, b_ in job:
                    P.dma(DMA(dst_, src_), ds_, writes=[b_], queue="pool")

    evac_rr = [0]

    def evac_copy(out_ap, in_ap, reads, writes):
        evac_rr[0] += 1
        if evac_rr[0] % 2:
            P.op("act", ACP(out_ap, in_ap), reads=reads, writes=writes)
        else:
            P.op("dve", CP(out_ap, in_ap), reads=reads, writes=writes)

    def norm_T(xin_, B_xin_, gbc, B_gbc, junk_, B_junk_, ss, B_ss, xnb_, B_xnb_, bank, dst, B_dst):
        P.op("act", ACTF(junk_, xin_, AF.Square, accum_out=ss), reads=[B_xin_], writes=[B_junk_, B_ss])
        P.op("act", ACTF(ss, ss, AF.Ln, bias=epsc, scale=1.0 / D), reads=[B_ss, B_const], writes=[B_ss])
        P.op("act", ACTF(ss, ss, AF.Exp, scale=-0.5), reads=[B_ss], writes=[B_ss])
        P.op("dve", STT(xnb_, xin_, ss, gbc, ALU.mult, ALU.mult), reads=[B_xin_, B_ss, B_gbc], writes=[B_xnb_])
        for kc in range(8):
            P.op("pe", TR(pbh[bank][:, kc * 128:(kc + 1) * 128], xnb_[:, kc * 128:(kc + 1) * 128], ident),
                 reads=[B_xnb_, B_const], writes=[PB[bank]])
        evac_copy(dst, pbh[bank][:, 0:1024].rearrange("p (k t) -> p k t", k=8), [PB[bank]], [B_dst])

    xt = [C([8, 512], BF16) for _ in range(2)]
    B_xt = [Buf() for _ in range(2)]
    ds_xt = [P.dsem("r0x%d" % i) for i in range(2)]
    Kz = [C([4, 1024], BF16) for _ in range(2)]
    B_Kz = [Buf() for _ in range(2)]
    Vr = [C([4, 2048], BF16) for _ in range(2)]
    B_Vr = [Buf() for _ in range(2)]
    g1bc = C([1, D], F32)
    B_g1 = Buf("g1bc")
    P.dma(DMA(g1bc, norm1_g.partition_broadcast(128)), newds(), writes=[B_g1])
    xin = [C([D], F32) for _ in range(6)]
    B_xin = [Buf("xin%d" % i) for i in range(6)]
    ds_xin = [P.dsem("xin%d" % i) for i in range(6)]
    junk = C([D], BF16)
    B_junk = Buf("junk")
    ssx = [C([1], F32) for _ in range(8)]
    B_ssx = [Buf() for _ in range(8)]
    xnb = [C([D], BF16) for _ in range(8)]
    B_xnb = [Buf() for _ in range(8)]
    stg = [C([8, 512], BF16) for _ in range(2)]
    B_stg = [Buf() for _ in range(2)]
    ds_stg = [P.dsem("stg%d" % i) for i in range(2)]
    XN = [Buf("xnT_d%d" % i) for i in range(16)]
    xnT_v = xnT_d.rearrange("(k p) t -> p k t", p=128)
    w_in_v = w_in_b.rearrange("(k p) c -> p k c", p=128)
    for hc in range(8):
        P.op("pool", MS(S32[:, hc, :], 0.0), writes=[B_S32[hc]])
    WATT = [Buf("watt%d" % j) for j in range(4)]
    ds_watt = [P.dsem("watt%d" % j) for j in range(4)]
    def watt_cast(j):
        for i3, c0 in enumerate((C_QA, C_KA, C_VA)):
            wsrc = w_in_f[:, :, c0:c0 + 1536].rearrange("p k (g j c) -> p k g j c", g=3, j=4)[:, :, :, j, :]
            for kc in range(8):
                o0 = (i3 * 8 + kc) * 384
                P.dma(DMA(w_att_b[j][:, o0:o0 + 384].rearrange("p (g c) -> p g c", g=3), wsrc[:, kc]), ds_watt[j],
                      reads=[XN[2 * j + 4]], writes=[WATT[j]], queue="pool")


    def x_part1(tt):
        src = x_prev if tt < 32 else x_main
        r0 = (tt % 32) * 128
        sl = tt % 6
        n2 = tt % 8
        P.dma(DMA(xin[sl], src[r0:r0 + 128, :]), ds_xin[sl], writes=[B_xin[sl]])
        P.op("act", ACTF(junk, xin[sl], AF.Square, accum_out=ssx[n2]), reads=[B_xin[sl]], writes=[B_junk, B_ssx[n2]])
        P.op("act", ACTF(ssx[n2], ssx[n2], AF.Ln, bias=epsc, scale=1.0 / D), reads=[B_ssx[n2], B_const], writes=[B_ssx[n2]])
        P.op("act", ACTF(ssx[n2], ssx[n2], AF.Exp, scale=-0.5), reads=[B_ssx[n2]], writes=[B_ssx[n2]])
        P.op("dve", STT(xnb[n2], xin[sl], ssx[n2], g1bc[:, 0, :], ALU.mult, ALU.mult),
             reads=[B_xin[sl], B_ssx[n2], B_g1], writes=[B_xnb[n2]])

    def x_part2(tt):
        grp, sub = tt // 4, tt % 4
        sg = grp % 2
        n2 = tt % 8
        bank = 4 + tt % 2
        for kc in range(8):
            P.op("pe", TR(pbh[bank][:, kc * 128:(kc + 1) * 128], xnb[n2][:, kc * 128:(kc + 1) * 128], ident),
                 reads=[B_xnb[n2], B_const], writes=[PB[bank]])
        evac_copy(stg[sg][:, :, sub * 128:(sub + 1) * 128], pbh[bank][:, 0:1024].rearrange("p (k t) -> p k t", k=8),
                  [PB[bank]], [B_stg[sg]])
        if sub == 3:
            P.dma(DMA(xnT_v[:, :, grp * 512:(grp + 1) * 512], stg[sg]), ds_stg[sg], reads=[B_stg[sg]], writes=[XN[grp]])

    xq = {"next": 0, "pend": None}

    def x_step():
        if xq["pend"] is not None:
            x_part2(xq["pend"])
            xq["pend"] = None
        if xq["next"] < 64:
            x_part1(xq["next"])
            xq["pend"] = xq["next"]
            xq["next"] += 1

    mmrr = 0

    def r0_tile(i):
        nonlocal mmrr
        s2 = i % 2
        if i == 0:
            P.dma(DMA(xt[0], xnT_v[:, :, 0:512]), ds_xt[0], reads=[XN[0]], writes=[B_xt[0]])
        if i + 1 < 8:
            P.dma(DMA(xt[1 - s2], xnT_v[:, :, (i + 1) * 512:(i + 2) * 512]), ds_xt[1 - s2], reads=[XN[i + 1]], writes=[B_xt[1 - s2]])
        ng = 0
        for sub in range(4):
            for cg in range(6):
                bk = mmrr % 4
                mmrr += 1
                for kc in range(8):
                    P.op("pe", MM(pb[bk], xt[s2][:, kc, sub * 128:(sub + 1) * 128], wkv[:, kc, cg * 512:(cg + 1) * 512],
                                  kc == 0, kc == 7), reads=[B_xt[s2], B_wkv], writes=[PB[bk]])
                if cg < 2:
                    for hh in range(2):
                        h = cg * 2 + hh
                        P.op("dve", TS(Kz[s2][:, sub, h * 256:(h + 1) * 256], pb[bk][:, hh * 256:(hh + 1) * 256],
                                       zeta512[:, sub, h:h + 1], None, ALU.mult),
                             reads=[PB[bk], B_const], writes=[B_Kz[s2]])
                else:
                    evac_copy(Vr[s2][:, sub, (cg - 2) * 512:(cg - 1) * 512], pb[bk], [PB[bk]], [B_Vr[s2]])
                ng += 1
        for hc in range(8):
            h = hc // 2
            bk = 6 + hc % 2
            for sub in range(4):
                P.op("pe", MM(pb[bk], Kz[s2][:, sub, hc * 128:(hc + 1) * 128], Vr[s2][:, sub, h * 512:(h + 1) * 512],
                              sub == 0, sub == 3), reads=[B_Kz[s2], B_Vr[s2]], writes=[PB[bk]])
            P.op("dve", STT(S32[:, hc, :], S32[:, hc, :], _HC["g512"][h], pb[bk], ALU.mult, ALU.add),
                 reads=[PB[bk], B_S32[hc]], writes=[B_S32[hc]])

    def x_b1(b):
        for tt in range(8 * b, 8 * b + 8):
            x_part1(tt)

    def x_b2(b):
        for tt in range(8 * b, 8 * b + 8):
            x_part2(tt)

    x_b1(0)
    x_b2(0)
    x_b1(1)
    for i in range(8):
        r0_tile(i)
        if i + 1 < 8:
            x_b2(i + 1)
        if 1 <= i <= 4:
            watt_cast(i - 1)
        if i + 2 < 8:
            x_b1(i + 2)
    for hc in range(8):
        P.op("act", ACP(Sbf[:, hc, :], S32[:, hc, :]), reads=[B_S32[hc]], writes=[B_Sbf[hc]])
    P.barrier()
    if debug == "R0":
        dsd = P.dsem("dbg")
        P.dma(DMA(dbg_s.rearrange("p (a v) -> p a v", a=8), S32), dsd, reads=B_S32, writes=[Buf()])
        return finish()

    C = Carver(SBASE)
    W_A = C([3, 8, 3, 128], BF16)
    B_WA = Buf("W_A")
    ds_WA = P.dsem("W_A")
    abias = [C([3, 256], F32) for _ in range(2)]
    B_ab = [Buf("abias0"), Buf("abias1")]
    ds_ab = [P.dsem("abias0"), P.dsem("abias1")]
    xs = [C([8, 2048], BF16) for _ in range(2)]
    B_xs = [Buf() for _ in range(2)]
    ds_xs = [P.dsem("xs%d" % i) for i in range(2)]
    Kt = [[C([16, 128], BF16) for _ in range(2)] for _ in range(3)]
    Vt = [[C([16, 128], BF16) for _ in range(2)] for _ in range(3)]
    Qt = [C([16, 128], BF16) for _ in range(3)]
    B_Kt = [[Buf() for _ in range(2)] for _ in range(3)]
    B_Vt = [[Buf() for _ in range(2)] for _ in range(3)]
    B_Qt = [Buf() for _ in range(3)]
    sq_all = C([4, 512], BF16)
    sq = [sq_all[:, i, :] for i in range(4)]
    B_sq = [Buf() for _ in range(4)]
    oaT = sq_all.rearrange("p a b -> p (a b)")
    rin = [C([512], F32) for _ in range(2)]
    B_rin = [Buf() for _ in range(2)]
    sb32 = [C([256], F32) for _ in range(4)]
    B_sb = [Buf() for _ in range(4)]
    pT = [C([256], BF16) for _ in range(4)]
    B_pT = [Buf() for _ in range(4)]
    acc = C([2, 2048], F32)
    B_acc = Buf("acc")
    ds_oaT = P.dsem("oaT")
    OA = [[Buf() for _ in range(2)] for _ in range(4)]
    pairs = [(j, s) for j in range(4 if debug != "A1" else 1) for s in (1, 2, 3)]

    W_A_flat = W_A.rearrange("p a b c d -> p (a b c d)")

    def a_load_w(j):
        P.dma(DMA(W_A_flat, w_att_b[j]), ds_WA, reads=[WATT[j]], writes=[B_WA])
        P.dma(DMA(abias[j % 2], c_abias.rearrange("p (g j) c -> p g j c", g=3)[:, :, j, :]), ds_ab[j % 2], writes=[B_ab[j % 2]])

    def a_load_x(pi):
        s_ = pairs[pi][1]
        xl_ = pi % 2
        P.dma(DMA(xs[xl_], xnT_v[:, :, s_ * 2048:(s_ + 1) * 2048]), ds_xs[xl_],
              reads=[XN[4 * s_ + i] for i in range(4)], writes=[B_xs[xl_]])

    a_load_w(0)
    a_load_x(0)
    for pi, (j, s) in enumerate(pairs):
        halo = (s == 1)
        h = s % 2
        X = xs[pi % 2]
        B_X = B_xs[pi % 2]
        AB = abias[j % 2]
        B_AB = B_ab[j % 2]
        items = []
        vgroups = []
        for g in range(3):
            Dg = DIL[g]
            tss = [3] if (halo and g < 2) else [0, 1, 2, 3]
            for ts in tss:
                items.append((g, "k", ts))
                if not halo:
                    items.append((g, "q", ts))
            tiles = list(range(16 - Dg, 16)) if (halo and g < 2) else list(range(16))
            for b0 in range(0, len(tiles), 4):
                vgroups.append((g, tiles[b0:b0 + 4]))

        def stage0(ii):
            g, kind, ts = items[ii]
            bk = ii % 4
            i3 = 0 if kind == "q" else 1
            for kc in range(8):
                P.op("pe", MM(pb[bk], W_A[:, i3, kc, g, :], X[:, kc, ts * 512:(ts + 1) * 512], kc == 0, kc == 7),
                     reads=[B_WA, B_X], writes=[PB[bk]])
            P.op("act", ACTF(sq[bk], pb[bk], AF.Square), reads=[PB[bk]], writes=[B_sq[bk]])

        def stage1(ii):
            g, kind, ts = items[ii]
            Dg = DIL[g]
            hg = g * 4 + j
            bk = ii % 4
            qb = 4 + ii % 2
            r2 = ii % 2
            P.op("pe", MM(pb[qb], ones, sq[bk]), reads=[B_sq[bk], B_const], writes=[PB[qb]])
            P.op("act", ACTF(rin[r2], pb[qb], AF.Ln, bias=epsc, scale=1.0 / 128), reads=[PB[qb], B_const], writes=[B_rin[r2]])
            P.op("act", ACTF(rin[r2], rin[r2], AF.Exp, scale=-0.5), reads=[B_rin[r2]], writes=[B_rin[r2]])
            if kind == "q":
                dst_t, B_dst, gv = Qt[g], B_Qt[g], gq
            else:
                dst_t, B_dst, gv = Kt[g][h], B_Kt[g][h], gk
            if Dg == 1:
                dv = dst_t[:, ts * 4:(ts + 1) * 4, :]
                sv = pb[bk].rearrange("p (a l) -> p a l", a=4)
                rv = rin[r2].rearrange("p (a l) -> p a l", a=4)
            elif Dg == 4:
                dv = dst_t[:, ts * 4:(ts + 1) * 4, :]
                sv = pb[bk].rearrange("p (l r) -> p r l", r=4)
                rv = rin[r2].rearrange("p (l r) -> p r l", r=4)
            else:
                dv = dst_t[:, :, ts * 32:(ts + 1) * 32]
                sv = pb[bk].rearrange("p (l r) -> p r l", r=16)
                rv = rin[r2].rearrange("p (l r) -> p r l", r=16)
            P.op("dve", STT(dv, sv, gv[:, hg:hg + 1], rv, ALU.mult, ALU.mult),
                 reads=[PB[bk], B_rin[r2], B_const], writes=[B_dst])

        def vgroup(vi):
            g, grp_t = vgroups[vi]
            Dg = DIL[g]
            bk = 6 + vi % 2
            for ti, t in enumerate(grp_t):
                s_, r = t // Dg, t % Dg
                base = s_ * 128 * Dg + r
                for kc in range(8):
                    lv = X[:, kc, base:base + 127 * Dg + 1:Dg] if Dg > 1 else X[:, kc, base:base + 128]
                    P.op("pe", MM(pb[bk][:, ti * 128:(ti + 1) * 128], lv, W_A[:, 2, kc, g, :], kc == 0, kc == 7),
                         reads=[B_WA, B_X], writes=[PB[bk]])
            n = len(grp_t)
            evac_copy(Vt[g][h][:, grp_t[0]:grp_t[0] + n, :],
                      pb[bk][:, 0:n * 128].rearrange("p (a d) -> p a d", a=n), [PB[bk]], [B_Vt[g][h]])

        SK = 2
        vi = 0
        for step in range(len(items) + SK):
            if step < len(items):
                stage0(step)
            if vi < len(vgroups):
                vgroup(vi)
                vi += 1
            if step - SK >= 0:
                stage1(step - SK)
        while vi < len(vgroups):
            vgroup(vi)
            vi += 1
        if pi + 1 < len(pairs):
            if pairs[pi + 1][0] != j:
                a_load_w(pairs[pi + 1][0])
            a_load_x(pi + 1)
        issue_casts(2)
        if halo:
            continue
        units = [(g, t) for g in range(3) for t in range(16)]
        info = {}

        def front(ui):
            g, t = units[ui]
            Dg = DIL[g]
            bk = ui % 4
            if t - Dg >= 0:
                kp, vp, B_kp, B_vp, onp = Kt[g][h][:, t - Dg, :], Vt[g][h][:, t - Dg, :], B_Kt[g][h], B_Vt[g][h], ones
            else:
                tp = 16 + t - Dg
                kp, vp, B_kp, B_vp = Kt[g][1 - h][:, tp, :], Vt[g][1 - h][:, tp, :], B_Kt[g][1 - h], B_Vt[g][1 - h]
                onp = ones_h if s == 2 else ones
            kc_, vc_ = Kt[g][h][:, t, :], Vt[g][h][:, t, :]
            qv = Qt[g][:, t, :]
            P.op("pe", MM(pb[bk][:, 0:128], kp, qv), reads=[B_kp, B_Qt[g]], writes=[PB[bk]])
            P.op("pe", MM(pb[bk][:, 128:256], kc_, qv), reads=[B_Kt[g][h], B_Qt[g]], writes=[PB[bk]])
            P.op("dve", TT(sb32[bk], pb[bk][:, 0:256], AB[:, g, :], ALU.add), reads=[PB[bk], B_AB], writes=[B_sb[bk]])
            P.op("act", ACTF(pT[bk], sb32[bk], AF.Exp), reads=[B_sb[bk]], writes=[B_pT[bk]])
            info[ui] = (vp, vc_, onp, B_vp)

        def back(ui):
            g, t = units[ui]
            Dg = DIL[g]
            bk = ui % 4
            vp, vc_, onp, B_vp = info.pop(ui)
            P.op("pe", MM(pb[bk][:, 256:384], vp, pT[bk][:, 0:128], True, False), reads=[B_vp, B_pT[bk]], writes=[PB[bk]])
            P.op("pe", MM(pb[bk][:, 256:384], vc_, pT[bk][:, 128:256], False, True), reads=[B_Vt[g][h], B_pT[bk]], writes=[PB[bk]])
            P.op("pe", MM(pb[bk][:, 384:512], onp, pT[bk][:, 0:128], True, False), reads=[B_const, B_pT[bk]], writes=[PB[bk]])
            P.op("pe", MM(pb[bk][:, 384:512], ones, pT[bk][:, 128:256], False, True), reads=[B_const, B_pT[bk]], writes=[PB[bk]])
            s_, r = t // Dg, t % Dg
            base = s_ * 128 * Dg + r
            av = acc[:, :, base:base + 127 * Dg + 1:Dg] if Dg > 1 else acc[:, :, base:base + 128]
            pv = pb[bk][:, 256:512].rearrange("p (a q) -> p a q", a=2)
            if g == 0:
                P.op("dve", CP(av, pv), reads=[PB[bk]], writes=[B_acc])
            else:
                P.op("dve", TT(av, pv, av, ALU.add), reads=[PB[bk], B_acc], writes=[B_acc])

        for step in range(len(units) + SK):
            if step < len(units):
                front(step)
            if step - SK >= 0:
                back(step - SK)
        P.op("dve", RCP(acc[:, 1, :], acc[:, 1, :]), reads=[B_acc], writes=[B_acc])
        P.op("dve", TT(oaT, acc[:, 0, :], acc[:, 1, :], ALU.mult), reads=[B_acc], writes=B_sq)
        P.dma(DMA(oaT_d[j * 128:(j + 1) * 128, (s - 2) * 2048:(s - 1) * 2048], oaT), ds_oaT,
              reads=B_sq, writes=[OA[j][s - 2]])
    issue_casts(100)
    P.barrier()
    if debug in ("A", "A1"):
        return finish()

    C = Carver(SBASE)
    xtl = C([4, D], F32)
    B_xtl = Buf("xtl")
    ds_xtl = P.dsem("xtl")
    ds_x1 = P.dsem("x1st")
    xnt = [C([8, 512], BF16) for _ in range(2)]
    B_xnt = [Buf() for _ in range(2)]
    ds_xnt = [P.dsem("xnt%d" % i) for i in range(2)]
    oat = [C([4, 512], BF16) for _ in range(2)]
    B_oat = [Buf() for _ in range(2)]
    ds_oat = [P.dsem("oat%d" % i) for i in range(2)]
    NW = 4
    wbuf = [C([4096], BF16) for _ in range(NW)]
    B_wb = [Buf() for _ in range(NW)]
    ds_wb = [P.dsem("wb%d" % i) for i in range(NW)]
    qkT = C([16, 512], BF16)
    B_qT, B_kT = Buf("qT"), Buf("kT")
    KzB = C([4, 1024], BF16)
    B_KzB = Buf("KzB")
    VrB = C([4, 2048], BF16)
    B_VrB = Buf("VrB")
    gT = C([16, 512], BF16)
    B_gT = Buf("gT")
    oh = [C([512], BF16) for _ in range(4)]
    B_oh = [Buf() for _ in range(4)]
    PB7H = [Buf("pb7a"), Buf("pb7b")]
    orT = C([16, 512], BF16)
    B_orT = Buf("orT")
    pTr = [C([4, 128], BF16) for _ in range(2)]
    B_pTr = [Buf() for _ in range(2)]
    dpm = C([4, 128], F32)
    xibc = C([4, 512], F32)
    B_rc = Buf("retconst")
    ds_rc = newds()
    P.dma(DMA(dpm, c_dpm), ds_rc, writes=[B_rc])
    P.dma(DMA(xibc, c_xibc), ds_rc, writes=[B_rc])
    stats = [C([4, 6], F32) for _ in range(2)]
    mv = [C([4, 2], F32) for _ in range(2)]
    rstd = [C([4], F32) for _ in range(2)]
    nmr = [C([4], F32) for _ in range(2)]
    B_st = [[Buf() for _ in range(4)] for _ in range(2)]
    B_rs = [Buf() for _ in range(2)]
    B_nm = [Buf() for _ in range(2)]
    t32 = [C([4, 128], F32) for _ in range(2)]
    B_t32 = [Buf() for _ in range(2)]
    ya32 = [C([512], F32) for _ in range(2)]
    B_ya = [Buf() for _ in range(2)]
    yb32 = [C([512], F32) for _ in range(2)]
    B_yb = [Buf() for _ in range(2)]
    sgT = qkT
    yT = KzB.rearrange("p a b -> p (a b)").rearrange("p (a b) -> p a b", a=8)
    X1 = [Buf("x1_%d" % i) for i in range(8)]
    wa_v = wa_b.rearrange("(k p) c -> p k c", p=128)
    wb_v = wb_b.rearrange("(k p) c -> p k c", p=128)
    wout_v = wout_b.rearrange("(k p) c -> p k c", p=128)
    wcnt = 0
    bkrr = 0
    wav = C([4, 1024], BF16)
    B_wa = Buf("wa_s")
    P.dma(DMA(wav, wa_v), newds(), reads=[WB["wa"]], writes=[B_wa])

    wpre = {}

    def wload(src_v, KC, c0, ncols, wbuf_name, prefetch=False):
        nonlocal wcnt
        key = (wbuf_name, c0)
        if not prefetch and key in wpre:
            return wpre.pop(key)
        sl = wcnt % NW
        wcnt += 1
        view = wbuf[sl].rearrange("p (k c) -> p k c", k=KC)
        P.dma(DMA(view, src_v[:, :, c0:c0 + ncols]), ds_wb[sl], reads=[WB[wbuf_name if wbuf_name != "w_in" else "w_in_rest"]], writes=[B_wb[sl]])
        if prefetch:
            wpre[key] = (view, B_wb[sl])
        return view, B_wb[sl]

    def wload_wb(og_):
        nonlocal wcnt
        sl = wcnt % NW
        wcnt += 1
        P.dma(DMA(wbuf[sl], wb_t[og_]), ds_wb[sl], reads=[WB["wb"]], writes=[B_wb[sl]])
        return wbuf[sl].rearrange("p (k c) -> p k c", k=16), B_wb[sl]

    def tile_loads(i_):
        tok0_ = i_ * 512
        s2_ = i_ % 2
        P.dma(DMA(xnt[s2_], xnT_v[:, :, HALF + tok0_:HALF + tok0_ + 512]), ds_xnt[s2_], reads=[XN[8 + i_]], writes=[B_xnt[s2_]])
        P.dma(DMA(oat[s2_], oaT_d.rearrange("(k p) t -> p k t", p=128)[:, :, tok0_:tok0_ + 512]), ds_oat[s2_],
              reads=[OA[jj][i_ // 4] for jj in range(4)], writes=[B_oat[s2_]])

    tile_loads(0)

    def nextbank():
        nonlocal bkrr
        b = bkrr % 3
        bkrr += 1
        return b

    import os
    KN = os.environ.get('KN', '')
    for i in range(8 if 'T1' not in KN else 1):
        tok0 = i * 512
        s2 = i % 2
        P.dma(DMA(xtl, x_main[tok0:tok0 + 512, :].rearrange("(s p) f -> p s f", p=128)), ds_xtl, writes=[B_xtl])
        XT, B_XT = xnt[s2], B_xnt[s2]

        def fm_group(c0, evac):
            wv, B_w = wload(w_in_v, 8, c0, 512, "w_in")
            for blk in range(4):
                bk = nextbank()
                for kc in range(8):
                    P.op("pe", MM(pb[bk], wv[:, kc, blk * 128:(blk + 1) * 128], XT[:, kc, :], kc == 0, kc == 7),
                         reads=[B_w, B_XT], writes=[PB[bk]])
                evac(blk, pb[bk], PB[bk])

        for grp in range(2):
            def ev_q(blk, ps, PBb, grp=grp):
                hc = grp * 4 + blk
                P.op("dve", TT(qkT[:, hc, :], ps, xibc[:, hc // 2, :], ALU.mult), reads=[PBb, B_rc], writes=[B_qT])
            fm_group(C_QR + grp * 512, ev_q)
        for grp in range(2):
            def ev_k(blk, ps, PBb, grp=grp):
                hc = grp * 4 + blk
                P.op("act", ACP(qkT[:, 8 + hc, :], ps), reads=[PBb], writes=[B_kT])
            fm_group(C_KR + grp * 512, ev_k)
        for sub in range(4):
            for hc in range(8):
                P.op("pe", TR(pbh[7][:, hc * 128:(hc + 1) * 128], qkT[:, 8 + hc, sub * 128:(sub + 1) * 128], ident),
                     reads=[B_kT, B_const], writes=[PB[7]])
            for h in range(4):
                P.op("dve", TS(KzB[:, sub, h * 256:(h + 1) * 256], pbh[7][:, h * 256:(h + 1) * 256],
                               zeta16[:, h:h + 1], None, ALU.mult), reads=[PB[7], B_const], writes=[B_KzB])
        for grp in range(4):
            wv, B_w = wload(w_in_v, 8, C_VR + grp * 512, 512, "w_in")
            for sub in range(4):
                bk = nextbank()
                for kc in range(8):
                    P.op("pe", MM(pb[bk], XT[:, kc, sub * 128:(sub + 1) * 128], wv[:, kc, :], kc == 0, kc == 7),
                         reads=[B_w, B_XT], writes=[PB[bk]])
                evac_copy(VrB[:, sub, grp * 512:(grp + 1) * 512], pb[bk], [PB[bk]], [B_VrB])
        for grp in range(4):
            def ev_g(blk, ps, PBb, grp=grp):
                P.op("act", ACTF(gT[:, grp * 4 + blk, :], ps, AF.Silu), reads=[PBb], writes=[B_gT])
            fm_group(C_GR + grp * 512, ev_g)
        def emit_A(sub):
            tsl_ = slice(sub * 128, (sub + 1) * 128)
            pr_ = sub % 2
            for h in range(4):
                for c in range(2):
                    hc = h * 2 + c
                    P.op("pe", MM(pb[0][:, h * 128:(h + 1) * 128], qkT[:, 8 + hc, tsl_], qkT[:, hc, tsl_], c == 0, c == 1),
                         reads=[B_kT, B_qT], writes=[PB[0]])
            P.op("dve", TT(pTr[pr_], pb[0].rearrange("p (h q) -> p h q", h=4), dpm, ALU.mult),
                 reads=[PB[0], B_rc], writes=[B_pTr[pr_]])

        emit_A(0)
        for sub in range(4):
            tsl = slice(sub * 128, (sub + 1) * 128)
            pr = sub % 2
            if sub > 0 and 'NOA' in KN:
                emit_A(sub)
            for h in range(4):
                ob = 1 + h
                P.op("pe", MM(pb[ob], pTr[pr][:, h, :], VrB[:, sub, h * 512:(h + 1) * 512], True, False),
                     reads=[B_pTr[pr], B_VrB], writes=[PB[ob]])
                for c in range(2):
                    hc = h * 2 + c
                    P.op("pe", MM(pb[ob], qkT[:, hc, tsl], Sbf[:, hc, :], False, c == 1),
                         reads=[B_qT, B_Sbf[hc]], writes=[PB[ob]])
            for h in range(4):
                ob = 1 + h
                P.op("dve", lambda e, o_=stats[pr][:, h, :], i_=pb[ob]: e.bn_stats(out=o_, in_=i_),
                     reads=[PB[ob]], writes=[B_st[pr][h]])
                P.op("dve", lambda e, o_=mv[pr][:, h, :], i_=stats[pr][:, h, :]: e.bn_aggr(out=o_, in_=i_),
                     reads=[B_st[pr][h]], writes=[B_st[pr][h]])
            P.op("act", ACTF(rstd[pr], mv[pr][:, :, 1], AF.Ln, bias=epsc, scale=1.0),
                 reads=B_st[pr] + [B_const], writes=[B_rs[pr]])
            P.op("act", ACTF(rstd[pr], rstd[pr], AF.Exp, scale=-0.5), reads=[B_rs[pr]], writes=[B_rs[pr]])
            P.op("dve", STT(nmr[pr], mv[pr][:, :, 0], -1.0, rstd[pr], ALU.mult, ALU.mult),
                 reads=B_st[pr] + [B_rs[pr]], writes=[B_nm[pr]])
            for h in range(4):
                ob = 1 + h
                P.op("act", ACTF(oh[h], pb[ob], AF.Identity, bias=nmr[pr][:, h:h + 1], scale=rstd[pr][:, h:h + 1]),
                     reads=[PB[ob], B_rs[pr], B_nm[pr]], writes=[B_oh[h]])
            for hc in range(8):
                h = hc // 2
                sbk = 5 if hc % 2 else 0
                P.op("pe", MM(pb[sbk], KzB[:, sub, hc * 128:(hc + 1) * 128], VrB[:, sub, h * 512:(h + 1) * 512]),
                     reads=[B_KzB, B_VrB], writes=[PB[sbk]])
                P.op("dve", STT(S32[:, hc, :], S32[:, hc, :], _HC["g128"][h], pb[sbk], ALU.mult, ALU.add),
                     reads=[PB[sbk], B_S32[hc]], writes=[B_S32[hc]])
                P.op("act", ACP(Sbf[:, hc, :], S32[:, hc, :]), reads=[B_S32[hc]], writes=[B_Sbf[hc]])
            if sub < 3 and 'NOA' not in KN:
                emit_A(sub + 1)
            for h in range(4):
                tb_ = 6 + h % 2
                for fc in range(4):
                    P.op("pe", TR(pbh[tb_][:, fc * 128:(fc + 1) * 128], oh[h][:, fc * 128:(fc + 1) * 128], ident),
                         reads=[B_oh[h], B_const], writes=[PB[tb_]])
                for fc in range(4):
                    f = h * 4 + fc
                    if fc < 3:
                        P.op("act", ACTF(t32[h % 2][:, fc, :], pbh[tb_][:, fc * 128:(fc + 1) * 128], AF.Identity,
                                         bias=gnb[:, f:f + 1], scale=gng[:, f:f + 1]),
                             reads=[PB[tb_], B_const], writes=[B_t32[h % 2]])
                    else:
                        P.op("dve", TS(t32[h % 2][:, fc, :], pbh[tb_][:, fc * 128:(fc + 1) * 128],
                                       gng[:, f:f + 1], gnb[:, f:f + 1], ALU.mult, ALU.add),
                             reads=[PB[tb_], B_const], writes=[B_t32[h % 2]])
                P.op("pool" if h < 3 else "dve", TT(orT[:, h * 4:(h + 1) * 4, tsl], t32[h % 2], gT[:, h * 4:(h + 1) * 4, tsl], ALU.mult),
                     reads=[B_t32[h % 2], B_gT], writes=[B_orT])
        for grp in range(4):
            def ev_s(blk, ps, PBb, grp=grp):
                P.op("act", ACTF(sgT[:, grp * 4 + blk, :], ps, AF.Sigmoid), reads=[PBb], writes=[B_qT, B_kT])
            fm_group(C_GA + grp * 512, ev_s)
        for og in range(4):
            wbv, B_wbv = wload_wb(og)
            for o2 in range(2):
                oc = og * 2 + o2
                y2 = oc % 2
                bka = nextbank()
                for kc in range(4):
                    P.op("pe", MM(pb[bka], wav[:, kc, oc * 128:(oc + 1) * 128], oat[s2][:, kc, :], kc == 0, kc == 3),
                         reads=[B_wa, B_oat[s2]], writes=[PB[bka]])
                P.op("dve", TT(ya32[y2], pb[bka], sgT[:, oc, :], ALU.mult), reads=[PB[bka], B_qT, B_kT], writes=[B_ya[y2]])
                bkb = nextbank()
                for kc in range(16):
                    P.op("pe", MM(pb[bkb], wbv[:, kc, o2 * 128:(o2 + 1) * 128], orT[:, kc, :], kc == 0, kc == 15),
                         reads=[B_wbv, B_orT], writes=[PB[bkb]])
                P.op("dve", TT(yb32[y2], pb[bkb], sgT[:, 8 + oc, :], ALU.mult), reads=[PB[bkb], B_qT, B_kT], writes=[B_yb[y2]])
                P.op("pool", TT(yT[:, oc, :], ya32[y2], yb32[y2], ALU.add), reads=[B_ya[y2], B_yb[y2]], writes=[B_KzB])
        if dbg and i == 0:
            dsd = P.dsem("dbg")
            P.dma(DMA(dbg_or.rearrange("p (a v) -> p a v", a=16), orT), dsd, reads=[B_orT], writes=[Buf()])
            P.dma(DMA(dbg_y.rearrange("p (a v) -> p a v", a=8), yT), dsd, reads=[B_KzB], writes=[Buf()])
            P.dma(DMA(dbg_sg.rearrange("p (a v) -> p a v", a=16), sgT), dsd, reads=[B_qT, B_kT], writes=[Buf()])
        for grp in range(2):
            wv, B_w = wload(wout_v, 8, grp * 512, 512, "wout")
            for sub in range(4):
                bk = nextbank()
                for kc in range(8):
                    P.op("pe", MM(pb[bk], yT[:, kc, sub * 128:(sub + 1) * 128], wv[:, kc, :], kc == 0, kc == 7),
                         reads=[B_w, B_KzB], writes=[PB[bk]])
                P.op("dve", TT(xtl[:, sub, grp * 512:(grp + 1) * 512], pb[bk], xtl[:, sub, grp * 512:(grp + 1) * 512], ALU.add),
                     reads=[PB[bk], B_xtl], writes=[B_xtl])
        if i + 1 < 8 and 'T1' not in KN:
            tile_loads(i + 1)
            wload(w_in_v, 8, C_QR, 512, "w_in", prefetch=True)
            wload(w_in_v, 8, C_QR + 512, 512, "w_in", prefetch=True)
        P.dma(DMA(out[tok0:tok0 + 512, :].rearrange("(s p) f -> p s f", p=128), xtl), ds_x1, reads=[B_xtl], writes=[X1[i]])
    P.barrier()
    if debug == "B1":
        return finish()

    C = Carver(GBASE)
    wup_s = C([8, 4096], BF16)
    wdn_s = C([32, 1024], BF16)
    B_wup, B_wdn = Buf("wup"), Buf("wdn")
    ds_w2 = P.dsem("w2")
    ds_w2b = P.dsem("w2b")
    wup_v = wup_b.rearrange("(k p) c -> p k c", p=128)
    wdn_v = wdown_b.rearrange("(k p) c -> p k c", p=128)
    B_wupq = [Buf("wup%d" % q) for q in range(4)]
    ds_wupq = [P.dsem("wupq%d" % q) for q in range(4)]
    for q4 in range(4):
        P.dma(DMA(wup_s[:, :, q4 * 1024:(q4 + 1) * 1024], wup_v[:, :, q4 * 1024:(q4 + 1) * 1024]), ds_wupq[q4],
              reads=[WB["wup"]], writes=[B_wupq[q4]])
    for q4 in range(4):
        P.dma(DMA(wdn_s[:, q4 * 8:(q4 + 1) * 8, :], wdn_v[:, q4 * 8:(q4 + 1) * 8, :]), ds_w2b,
              reads=[WB["wdown"]], writes=[B_wdn])
    g2bc = C([1, D], F32)
    B_g2 = Buf("g2bc")
    P.dma(DMA(g2bc, norm2_g.partition_broadcast(128)), newds(), writes=[B_g2])
    x1t = [C([2, D], F32) for _ in range(3)]
    B_x1t = [Buf() for _ in range(3)]
    ds_x1t = [P.dsem("x1t%d" % i) for i in range(3)]
    ds_ot = [P.dsem("ot%d" % i) for i in range(3)]
    junk2 = C([D], BF16)
    B_junk2 = Buf()
    ss2 = [C([1], F32) for _ in range(2)]
    B_ss2 = [Buf() for _ in range(2)]
    xnb2 = [C([D], BF16) for _ in range(2)]
    B_xnb2 = [Buf() for _ in range(2)]
    xn2T = [C([8, 256], BF16) for _ in range(2)]
    B_xn2T = [Buf() for _ in range(2)]
    hT = [C([32, 256], BF16) for _ in range(2)]
    B_hT = [Buf("hT0"), Buf("hT1")]
    rl = [C([256], BF16) for _ in range(2)]
    B_rl = [Buf() for _ in range(2)]
    OUTB = [Buf() for _ in range(16)]

    def b2_norm1(i):
        tok0 = i * 256
        s3 = i % 3
        P.dma(DMA(x1t[s3], out[tok0:tok0 + 256, :].rearrange("(s p) f -> p s f", p=128)), ds_x1t[s3],
              reads=[X1[i // 2]], writes=[B_x1t[s3]])
        for sub in range(2):
            xi_ = x1t[s3][:, sub, :]
            P.op("act", ACTF(junk2, xi_, AF.Square, accum_out=ss2[sub]), reads=[B_x1t[s3]], writes=[B_junk2, B_ss2[sub]])
            P.op("act", ACTF(ss2[sub], ss2[sub], AF.Ln, bias=epsc, scale=1.0 / D), reads=[B_ss2[sub], B_const], writes=[B_ss2[sub]])
            P.op("act", ACTF(ss2[sub], ss2[sub], AF.Exp, scale=-0.5), reads=[B_ss2[sub]], writes=[B_ss2[sub]])
            P.op("dve", STT(xnb2[sub], xi_, ss2[sub], g2bc[:, 0, :], ALU.mult, ALU.mult),
                 reads=[B_x1t[s3], B_ss2[sub], B_g2], writes=[B_xnb2[sub]])

    def b2_norm2(i):
        s2 = i % 2
        for sub in range(2):
            for kc in range(8):
                P.op("pe", TR(pbh[sub][:, kc * 128:(kc + 1) * 128], xnb2[sub][:, kc * 128:(kc + 1) * 128], ident),
                     reads=[B_xnb2[sub], B_const], writes=[PB[sub]])
            evac_copy(xn2T[s2][:, :, sub * 128:(sub + 1) * 128], pbh[sub][:, 0:1024].rearrange("p (k t) -> p k t", k=8),
                      [PB[sub]], [B_xn2T[s2]])

    def b2_up(i):
        s2 = i % 2
        for fc in range(32):
            bk = 2 + fc % 3
            for kc in range(8):
                P.op("pe", MM(pb[bk][:, 0:256], wup_s[:, kc, fc * 128:(fc + 1) * 128], xn2T[s2][:, kc, :], kc == 0, kc == 7),
                     reads=[B_wupq[fc // 8], B_xn2T[s2]], writes=[PB[bk]])
            r2 = fc % 2
            P.op("act", ACTF(rl[r2], pb[bk][:, 0:256], AF.Relu), reads=[PB[bk]], writes=[B_rl[r2]])
            P.op("pool", TT(hT[s2][:, fc, :], rl[r2], rl[r2], ALU.mult), reads=[B_rl[r2]], writes=[B_hT[s2]])

    def b2_down(i):
        tok0 = i * 256
        s2 = i % 2
        s3 = i % 3
        for sub in range(2):
            for grp in range(2):
                bk = 5 + (sub * 2 + grp) % 3
                for kc in range(32):
                    P.op("pe", MM(pb[bk], hT[s2][:, kc, sub * 128:(sub + 1) * 128], wdn_s[:, kc, grp * 512:(grp + 1) * 512],
                                  kc == 0, kc == 31), reads=[B_hT[s2], B_wdn], writes=[PB[bk]])
                P.op("dve", TT(x1t[s3][:, sub, grp * 512:(grp + 1) * 512], pb[bk], x1t[s3][:, sub, grp * 512:(grp + 1) * 512],
                               ALU.add), reads=[PB[bk], B_x1t[s3]], writes=[B_x1t[s3]])
        P.dma(DMA(out[tok0:tok0 + 256, :].rearrange("(s p) f -> p s f", p=128), x1t[s3]), ds_ot[s3],
              reads=[B_x1t[s3]], writes=[OUTB[i]])

    b2_norm1(0)
    b2_norm2(0)
    for i in range(16):
        b2_up(i)
        if i + 1 < 16:
            b2_norm1(i + 1)
        if i >= 1:
            b2_down(i - 1)
        if i + 1 < 16:
            b2_norm2(i + 1)
    b2_down(15)
    P.wait_bufs("sp", OUTB)
    return finish()


def _in_maps(inputs):
    x = np.ascontiguousarray(inputs["x"], dtype=np.float32)
    shared = {
        "norm1_g": inputs["norm1_g"].reshape(1, D), "norm2_g": inputs["norm2_g"].reshape(1, D),
        "w_in": inputs["w_in"].reshape(D, IN_W), "q_norm_g": inputs["q_norm_g"].reshape(12, 128),
        "k_norm_g": inputs["k_norm_g"].reshape(12, 128), "ret_gn_g": inputs["ret_gn_g"].reshape(1, 2048),
        "ret_gn_b": inputs["ret_gn_b"].reshape(1, 2048), "w_proj_a": inputs["w_proj_a"].reshape(512, D),
        "w_proj_b": inputs["w_proj_b"].reshape(2048, D), "w_out": inputs["w_out"].reshape(D, D),
        "w_up": inputs["w_up"].reshape(D, 4096), "w_down": inputs["w_down"].reshape(4096, D),
        "c_ident": _HC["ident"], "c_abias": _HC["abias"], "c_dpm": _HC["dpm"], "c_xibc": _HC["xibc"],
        "c_zeta16": _HC["zeta16"], "c_zeta512": _HC["zeta512"],
    }
    shared = {k: np.ascontiguousarray(v, dtype=np.float32) for k, v in shared.items()}
    maps = []
    zeros = np.zeros((HALF, D), np.float32)
    for c in range(8):
        b, half = c // 2, c % 2
        m = dict(shared)
        m["x_main"] = x[b, half * HALF:(half + 1) * HALF]
        m["x_prev"] = x[b, 0:HALF] if half else zeros
        m["hflag"] = np.full((128, 128), float(half), np.float32)
        maps.append(m)
    return maps


def kernel(**inputs):
    nc = build_nc()
    maps = _in_maps(inputs)
    res = run_bass_kernel_spmd(nc, maps, core_ids=list(range(8)))
    outp = np.empty((4, 2 * HALF, D), np.float32)
    for c in range(8):
        b, half = c // 2, c % 2
        outp[b, half * HALF:(half + 1) * HALF] = res.results[c]["out"]
    return outp
```

```python
import contextlib
import numpy as np
import concourse.bass as bass
import concourse.mybir as mybir
from concourse.bass_utils import run_bass_kernel_spmd

F32 = mybir.dt.float32
BF16 = mybir.dt.bfloat16
AF = mybir.ActivationFunctionType
ALU = mybir.AluOpType

COMPUTE = ("pe", "act", "dve", "pool")
QUEUES = ("sp",)
EPS = 1e-6
NEG = -30000.0


def MM(out, lhsT, rhs, start=True, stop=True):
    return lambda e: e.matmul(out, lhsT=lhsT, rhs=rhs, start=start, stop=stop)


def TR(out, in_, identity):
    return lambda e: e.transpose(out=out, in_=in_, identity=identity)


def ACTF(out, in_, func, **kw):
    return lambda e: e.activation(out=out, in_=in_, func=func, **kw)


def ACP(out, in_):
    return lambda e: e.copy(out=out, in_=in_)


def CP(out, in_):
    return lambda e: e.tensor_copy(out=out, in_=in_)


def TT(out, in0, in1, op):
    return lambda e: e.tensor_tensor(out=out, in0=in0, in1=in1, op=op)


def TS(out, in0, s1, s2, op0, op1=None):
    if op1 is None:
        return lambda e: e.tensor_scalar(out=out, in0=in0, scalar1=s1, scalar2=None, op0=op0)
    return lambda e: e.tensor_scalar(out=out, in0=in0, scalar1=s1, scalar2=s2, op0=op0, op1=op1)


def STT(out, in0, scalar, in1, op0, op1):
    return lambda e: e.scalar_tensor_tensor(out=out, in0=in0, scalar=scalar, in1=in1, op0=op0, op1=op1)


def MS(ap, val):
    return lambda e: e.memset(ap, val)


def RCP(out, in_):
    return lambda e: e.reciprocal(out=out, in_=in_)


def DMA(out, in_, **kw):
    return lambda e: e.dma_start(out=out, in_=in_, **kw)


class Buf:
    __slots__ = ("name", "lws", "reads")

    def __init__(self, name=""):
        self.name = name
        self.lws = {}
        self.reads = []


class DmaSem:
    __slots__ = ("sem", "val", "name")

    def __init__(self, name):
        self.name = name
        self.sem = None
        self.val = 0


class Prog:
    def __init__(self, nc):
        self.nc = nc
        self.ops = {e: [] for e in COMPUTE + QUEUES}
        self.seen = {e: {} for e in COMPUTE + QUEUES}
        self.dsems = []
        self.stack = contextlib.ExitStack()

    def dsem(self, name):
        d = DmaSem(name)
        self.dsems.append(d)
        return d

    def _need(self, eng, dep, waits, kind):
        if dep[0] == "eng":
            _, e2, idx = dep
            if e2 == eng and (eng == "pe" or kind != "raw"):
                return
            key = ("eng", e2)
            if self.seen[eng].get(key, -1) >= idx:
                return
            self.seen[eng][key] = idx
            self.ops[e2][idx][2] = True
            waits.append(dep)
        else:
            _, ds, val = dep
            key = ("dma", id(ds))
            if self.seen[eng].get(key, -1) >= val:
                return
            self.seen[eng][key] = val
            waits.append(dep)

    def _deps(self, eng, reads, writes):
        waits = []
        for b in reads:
            for d in b.lws.values():
                self._need(eng, d, waits, "raw")
        for b in writes:
            for d in b.lws.values():
                self._need(eng, d, waits, "waw")
            for r in b.reads:
                self._need(eng, r, waits, "war")
        return waits

    def _commit(self, me, key, reads, writes):
        for b in reads:
            b.reads.append(me)
        for b in writes:
            b.lws[key] = me
            b.reads = []

    def op(self, eng, fn, reads=(), writes=()):
        waits = self._deps(eng, reads, writes)
        idx = len(self.ops[eng])
        self.ops[eng].append([waits, fn, False, None])
        self._commit(("eng", eng, idx), eng, reads, writes)

    def dma(self, fn, ds, reads=(), writes=(), queue="sp"):
        waits = self._deps(queue, reads, writes)
        ds.val += 16
        self.ops[queue].append([waits, fn, False, ds])
        self._commit(("dma", ds, ds.val), ("dma", id(ds)), reads, writes)

    def barrier(self):
        lasts = {}
        for e in COMPUTE:
            for i in range(len(self.ops[e]) - 1, -1, -1):
                o = self.ops[e][i]
                if o[1] is not None and o[3] is None:
                    lasts[e] = i
                    break
        dvals = [(d, d.val) for d in self.dsems if d.val > 0]
        for eng in COMPUTE + QUEUES:
            waits = []
            for e2, idx in lasts.items():
                if e2 == eng and eng == "pe":
                    continue
                self._need(eng, ("eng", e2, idx), waits, "raw")
            for d, v in dvals:
                self._need(eng, ("dma", d, v), waits, "raw")
            self.ops[eng].append([waits, None, False, None])

    def wait_bufs(self, eng, bufs):
        waits = []
        for b in bufs:
            for d in b.lws.values():
                self._need(eng, d, waits, "raw")
        self.ops[eng].append([waits, None, False, None])

    def flush(self):
        nc = self.nc
        st = self.stack
        esem = {e: st.enter_context(nc.semaphore("s_" + e)) for e in COMPUTE}
        for d in self.dsems:
            d.sem = st.enter_context(nc.semaphore("d_" + d.name))
        sig = {}
        for e in COMPUTE:
            c = 0
            arr = []
            for o in self.ops[e]:
                if o[2]:
                    c += 1
                arr.append(c)
            sig[e] = arr
        block = st.enter_context(nc.Block())

        def emit(ename):
            def body(eng):
                for waits, fn, signaling, ds in self.ops[ename]:
                    for w in waits:
                        if w[0] == "eng":
                            eng.wait_ge(esem[w[1]], sig[w[1]][w[2]])
                        else:
                            eng.wait_ge(w[1].sem, w[2])
                    if fn is None:
                        continue
                    ins = fn(eng)
                    if ds is not None:
                        ins.then_inc(ds.sem, 16)
                    elif signaling:
                        ins.then_inc(esem[ename], 1)
            return body

        block.tensor(emit("pe"))
        block.scalar(emit("act"))
        block.vector(emit("dve"))
        block.gpsimd(emit("pool"))
        block.sync(emit("sp"))
        st.close()


D = 1024
HALF = 4096
IN_W = 12800
C_QA, C_KA, C_VA, C_QR, C_KR, C_VR, C_GR, C_GA, C_GB = 0, 1536, 3072, 4608, 5632, 6656, 8704, 10752, 11776
DIL = (1, 4, 16)
ARENA = 212480


def _host_consts():
    c = {}
    c["ident"] = np.eye(128, dtype=np.float32)
    slopes = 2.0 ** (-8.0 * np.arange(1, 13, dtype=np.float32) / 12)
    kj = np.arange(128)[:, None]
    qi = np.arange(128)[None, :]
    ab = np.empty((128, 12, 256), np.float32)
    for h in range(12):
        dil = DIL[h // 4]
        dprev = (128 + qi - kj).astype(np.float32)
        dcur = (qi - kj).astype(np.float32)
        ab[:, h, 0:128] = np.where(kj >= qi, -slopes[h] * dprev * dil, NEG)
        ab[:, h, 128:256] = np.where(kj <= qi, -slopes[h] * dcur * dil, NEG)
    c["abias"] = ab
    lg = np.log(1.0 - 2.0 ** (-5.0 - np.arange(4, dtype=np.float64)))
    p = np.arange(128, dtype=np.float64)
    dp = np.zeros((128, 4, 128), np.float64)
    for h in range(4):
        dp[:, h, :] = np.where(kj <= qi, np.exp(-lg[h] * (p[:, None] + 1.0)) / 16.0, 0.0)
    c["dpm"] = dp.astype(np.float32)
    xi = np.exp(lg[:, None] * (p[None, :] + 1.0))
    c["xibc"] = np.broadcast_to(np.tile(xi, (1, 4))[None], (128, 4, 512)).astype(np.float32).copy()
    c["zeta16"] = (np.exp(lg[None, :] * (127.0 - p[:, None])) / 16.0).astype(np.float32)
    z5 = np.empty((128, 4, 4), np.float64)
    for sub in range(4):
        z5[:, sub, :] = np.exp(lg[None, :] * (511.0 - (sub * 128 + p[:, None]))) / 16.0
    c["zeta512"] = z5.astype(np.float32)
    c["g128"] = [float(np.exp(lg[h] * 128.0)) for h in range(4)]
    c["g512"] = [float(np.exp(lg[h] * 512.0)) for h in range(4)]
    return c


_HC = _host_consts()


def build_nc(debug=False):
    nc = bass.Bass("TRN2", target_bir_lowering=False)

    def din(name, shape):
        return nc.dram_tensor(name, list(shape), F32, kind="ExternalInput").ap()

    def dscr(name, shape, dt, out=False):
        return nc.dram_tensor(name, list(shape), dt, kind="ExternalOutput" if out else "Internal").ap()

    x_main = din("x_main", [HALF, D])
    x_prev = din("x_prev", [HALF, D])
    hflag = din("hflag", [128, 128])
    norm1_g = din("norm1_g", [1, D])
    norm2_g = din("norm2_g", [1, D])
    w_in = din("w_in", [D, IN_W])
    q_norm_g = din("q_norm_g", [12, 128])
    k_norm_g = din("k_norm_g", [12, 128])
    ret_gn_g = din("ret_gn_g", [1, 2048])
    ret_gn_b = din("ret_gn_b", [1, 2048])
    w_proj_a = din("w_proj_a", [512, D])
    w_proj_b = din("w_proj_b", [2048, D])
    w_out = din("w_out", [D, D])
    w_up = din("w_up", [D, 4096])
    w_down = din("w_down", [4096, D])
    c_ident = din("c_ident", [128, 128])
    c_abias = din("c_abias", [128, 12, 256])
    c_dpm = din("c_dpm", [128, 4, 128])
    c_xibc = din("c_xibc", [128, 4, 512])
    c_zeta16 = din("c_zeta16", [128, 4])
    c_zeta512 = din("c_zeta512", [128, 4, 4])
    out = nc.dram_tensor("out", [HALF, D], F32, kind="ExternalOutput").ap()

    w_in_b = dscr("w_in_b", [D, IN_W], BF16)
    wa_b = dscr("wa_b", [512, D], BF16)
    wb_b = dscr("wb_b", [2048, D], BF16)
    wout_b = dscr("wout_b", [D, D], BF16)
    wup_b = dscr("wup_b", [D, 4096], BF16)
    wdown_b = dscr("wdown_b", [4096, D], BF16)
    wb_t = dscr("wb_t", [4, 128, 4096], BF16)
    w_att_b = dscr("w_att_b", [4, 128, 9216], BF16)
    dbg = bool(debug)
    xnT_d = dscr("xnT_d", [D, 2 * HALF], BF16, out=dbg)
    oaT_d = dscr("oaT_d", [512, HALF], BF16, out=dbg)
    dbg_s = dscr("dbg_s", [128, 8 * 512], F32, out=True) if dbg else None
    dbg_or = dscr("dbg_or", [128, 16 * 512], BF16, out=True) if dbg else None
    dbg_y = dscr("dbg_y", [128, 8 * 512], BF16, out=True) if dbg else None
    dbg_sg = dscr("dbg_sg", [128, 16 * 512], BF16, out=True) if dbg else None

    P = Prog(nc)
    st = P.stack
    arena = st.enter_context(nc.sbuf_tensor("arena", [128, ARENA // 2], BF16))
    pbt = [st.enter_context(nc.psum_tensor("pb%d" % i, [128, 512], F32)) for i in range(8)]
    pb = [t[:] for t in pbt]
    pbh = [t[:].bitcast(BF16) for t in pbt]
    PB = [Buf("pb%d" % i) for i in range(8)]
    letters = "abcdefg"

    class Carver:
        def __init__(self, base):
            self.off = base

        def __call__(self, shape, dt):
            n = int(np.prod(shape))
            esz = 4 if dt == F32 else 2
            nb = (n * esz + 31) // 32 * 32
            a = arena[:, self.off // 2: self.off // 2 + n * esz // 2]
            self.off += nb
            assert self.off <= ARENA, ("arena overflow", self.off)
            if dt == F32:
                a = a.bitcast(F32)
            if len(shape) > 1:
                names = " ".join(letters[i] for i in range(len(shape)))
                kw = {letters[i]: int(shape[i]) for i in range(len(shape))}
                a = a.rearrange("p (%s) -> p %s" % (names, names), **kw)
            return a

    def finish():
        P.barrier()
        P.flush()
        return nc

    G = Carver(0)
    ident = G([128], BF16)
    ones = G([128], BF16)
    ones_h = G([128], BF16)
    epsc = G([1], F32)
    mhalf = G([4], F32)
    gq = G([12], F32)
    gk = G([12], F32)
    gng = G([16], F32)
    gnb = G([16], F32)
    zeta16 = G([4], F32)
    zeta512 = G([4, 4], F32)
    tmpc = G([128], F32)
    tmpc2 = G([128], F32)
    B_const = Buf("const")
    B_tmpc = Buf("tmpc")
    B_tmpc2 = Buf("tmpc2")
    ds_c = P.dsem("const")
    dsn = [0]

    def newds():
        dsn[0] += 1
        return P.dsem("c%d" % dsn[0])
    GBASE = (G.off + 63) // 64 * 64

    P.dma(DMA(tmpc, c_ident), newds(), writes=[B_tmpc])
    P.dma(DMA(tmpc2, hflag), newds(), writes=[B_tmpc2])
    P.dma(DMA(gq, q_norm_g.rearrange("h d -> d h"), allow_slow_non_contiguous=True), ds_c, writes=[B_const])
    P.dma(DMA(gk, k_norm_g.rearrange("h d -> d h"), allow_slow_non_contiguous=True), ds_c, writes=[B_const])
    P.dma(DMA(gng, ret_gn_g.rearrange("o (c p) -> p (o c)", p=128), allow_slow_non_contiguous=True), ds_c, writes=[B_const])
    P.dma(DMA(gnb, ret_gn_b.rearrange("o (c p) -> p (o c)", p=128), allow_slow_non_contiguous=True), ds_c, writes=[B_const])
    P.dma(DMA(zeta16, c_zeta16), ds_c, writes=[B_const])
    P.dma(DMA(zeta512, c_zeta512), ds_c, writes=[B_const])
    P.op("dve", CP(ident, tmpc), reads=[B_tmpc], writes=[B_const])
    P.op("dve", CP(ones_h, tmpc2), reads=[B_tmpc2], writes=[B_const])
    P.op("dve", MS(ones, 1.0), writes=[B_const])
    P.op("dve", MS(epsc, EPS), writes=[B_const])
    P.op("dve", MS(mhalf, -0.5), writes=[B_const])
    P.op("dve", TS(gq, gq, float(128.0 ** -0.5), None, ALU.mult), reads=[B_const], writes=[B_const])

    C = Carver(GBASE)
    S32 = C([8, 512], F32)
    Sbf = C([8, 512], BF16)
    B_S32 = [Buf("S32_%d" % i) for i in range(8)]
    B_Sbf = [Buf("Sbf_%d" % i) for i in range(8)]
    SBASE = (C.off + 63) // 64 * 64
    C = Carver(SBASE)
    wkv = C([8, 3072], BF16)
    B_wkv = Buf("wkv")
    ds_wkv = P.dsem("wkv")
    w_in_f = w_in.rearrange("(k p) c -> p k c", p=128)
    for i in range(6):
        P.dma(DMA(wkv[:, :, i * 512:(i + 1) * 512], w_in_f[:, :, C_KR + i * 512:C_KR + (i + 1) * 512]),
              ds_wkv, writes=[B_wkv], queue="pool")
    WB = {}
    cast_jobs = []
    for name, src, dst, rows, nch, c0, c1 in (
        ("w_in_rest", w_in, w_in_b, D, 8, C_QR, IN_W),
        ("wa", w_proj_a, wa_b, 512, 1, 0, D),
        ("wout", w_out, wout_b, D, 1, 0, D), ("wup", w_up, wup_b, D, 4, 0, 4096), ("wdown", w_down, wdown_b, 4096, 4, 0, D),
    ):
        b = Buf(name)
        ds = P.dsem("w_" + name)
        step = rows // nch
        for i in range(nch):
            cast_jobs.append((dst[i * step:(i + 1) * step, c0:c1], src[i * step:(i + 1) * step, c0:c1], ds, b))
        WB[name] = b

    b_wbt = Buf("wb")
    ds_wbt = P.dsem("w_wb")
    for og_ in range(4):
        cast_jobs.append([(wb_t[og_][:, kc_ * 256:(kc_ + 1) * 256],
                           w_proj_b[kc_ * 128:(kc_ + 1) * 128, og_ * 256:(og_ + 1) * 256], ds_wbt, b_wbt) for kc_ in range(16)])
    WB["wb"] = b_wbt

    def issue_casts(n):
        for _ in range(n):
            if cast_jobs:
                job = cast_jobs.pop(0)
                if not isinstance(job, list):
                    job = [job]
                for dst_, src_, ds_, b_ in job:
                    P.dma(DMA(dst_, src_), ds_, writes=[b_], queue="pool")

    evac_rr = [0]

    def evac_copy(out_ap, in_ap, reads, writes):
        evac_rr[0] += 1
        if evac_rr[0] % 2:
            P.op("act", ACP(out_ap, in_ap), reads=reads, writes=writes)
        else:
            P.op("dve", CP(out_ap, in_ap), reads=reads, writes=writes)

    def norm_T(xin_, B_xin_, gbc, B_gbc, junk_, B_junk_, ss, B_ss, xnb_, B_xnb_, bank, dst, B_dst):
        P.op("act", ACTF(junk_, xin_, AF.Square, accum_out=ss), reads=[B_xin_], writes=[B_junk_, B_ss])
        P.op("act", ACTF(ss, ss, AF.Ln, bias=epsc, scale=1.0 / D), reads=[B_ss, B_const], writes=[B_ss])
        P.op("act", ACTF(ss, ss, AF.Exp, scale=-0.5), reads=[B_ss], writes=[B_ss])
        P.op("dve", STT(xnb_, xin_, ss, gbc, ALU.mult, ALU.mult), reads=[B_xin_, B_ss, B_gbc], writes=[B_xnb_])
        for kc in range(8):
            P.op("pe", TR(pbh[bank][:, kc * 128:(kc + 1) * 128], xnb_[:, kc * 128:(kc + 1) * 128], ident),
                 reads=[B_xnb_, B_const], writes=[PB[bank]])
        evac_copy(dst, pbh[bank][:, 0:1024].rearrange("p (k t) -> p k t", k=8), [PB[bank]], [B_dst])

    xt = [C([8, 512], BF16) for _ in range(2)]
    B_xt = [Buf() for _ in range(2)]
    ds_xt = [P.dsem("r0x%d" % i) for i in range(2)]
    Kz = [C([4, 1024], BF16) for _ in range(2)]
    B_Kz = [Buf() for _ in range(2)]
    Vr = [C([4, 2048], BF16) for _ in range(2)]
    B_Vr = [Buf() for _ in range(2)]
    g1bc = C([1, D], F32)
    B_g1 = Buf("g1bc")
    P.dma(DMA(g1bc, norm1_g.partition_broadcast(128)), newds(), writes=[B_g1])
    xin = [C([D], F32) for _ in range(6)]
    B_xin = [Buf("xin%d" % i) for i in range(6)]
    ds_xin = [P.dsem("xin%d" % i) for i in range(6)]
    junk = C([D], BF16)
    B_junk = Buf("junk")
    ssx = [C([1], F32) for _ in range(8)]
    B_ssx = [Buf() for _ in range(8)]
    xnb = [C([D], BF16) for _ in range(8)]
    B_xnb = [Buf() for _ in range(8)]
    stg = [C([8, 512], BF16) for _ in range(2)]
    B_stg = [Buf() for _ in range(2)]
    ds_stg = [P.dsem("stg%d" % i) for i in range(2)]
    XN = [Buf("xnT_d%d" % i) for i in range(16)]
    xnT_v = xnT_d.rearrange("(k p) t -> p k t", p=128)
    w_in_v = w_in_b.rearrange("(k p) c -> p k c", p=128)
    for hc in range(8):
        P.op("pool", MS(S32[:, hc, :], 0.0), writes=[B_S32[hc]])
    WATT = [Buf("watt%d" % j) for j in range(4)]
    ds_watt = [P.dsem("watt%d" % j) for j in range(4)]
    def watt_cast(j):
        for i3, c0 in enumerate((C_QA, C_KA, C_VA)):
            wsrc = w_in_f[:, :, c0:c0 + 1536].rearrange("p k (g j c) -> p k g j c", g=3, j=4)[:, :, :, j, :]
            for kc in range(8):
                o0 = (i3 * 8 + kc) * 384
                P.dma(DMA(w_att_b[j][:, o0:o0 + 384].rearrange("p (g c) -> p g c", g=3), wsrc[:, kc]), ds_watt[j],
                      reads=[XN[2 * j + 4]], writes=[WATT[j]], queue="pool")


    def x_part1(tt):
        src = x_prev if tt < 32 else x_main
        r0 = (tt % 32) * 128
        sl = tt % 6
        n2 = tt % 8
        P.dma(DMA(xin[sl], src[r0:r0 + 128, :]), ds_xin[sl], writes=[B_xin[sl]])
        P.op("act", ACTF(junk, xin[sl], AF.Square, accum_out=ssx[n2]), reads=[B_xin[sl]], writes=[B_junk, B_ssx[n2]])
        P.op("act", ACTF(ssx[n2], ssx[n2], AF.Ln, bias=epsc, scale=1.0 / D), reads=[B_ssx[n2], B_const], writes=[B_ssx[n2]])
        P.op("act", ACTF(ssx[n2], ssx[n2], AF.Exp, scale=-0.5), reads=[B_ssx[n2]], writes=[B_ssx[n2]])
        P.op("dve", STT(xnb[n2], xin[sl], ssx[n2], g1bc[:, 0, :], ALU.mult, ALU.mult),
             reads=[B_xin[sl], B_ssx[n2], B_g1], writes=[B_xnb[n2]])

    def x_part2(tt):
        grp, sub = tt // 4, tt % 4
        sg = grp % 2
        n2 = tt % 8
        bank = 4 + tt % 2
        for kc in range(8):
            P.op("pe", TR(pbh[bank][:, kc * 128:(kc + 1) * 128], xnb[n2][:, kc * 128:(kc + 1) * 128], ident),
                 reads=[B_xnb[n2], B_const], writes=[PB[bank]])
        evac_copy(stg[sg][:, :, sub * 128:(sub + 1) * 128], pbh[bank][:, 0:1024].rearrange("p (k t) -> p k t", k=8),
                  [PB[bank]], [B_stg[sg]])
        if sub == 3:
            P.dma(DMA(xnT_v[:, :, grp * 512:(grp + 1) * 512], stg[sg]), ds_stg[sg], reads=[B_stg[sg]], writes=[XN[grp]])

    xq = {"next": 0, "pend": None}

    def x_step():
        if xq["pend"] is not None:
            x_part2(xq["pend"])
            xq["pend"] = None
        if xq["next"] < 64:
            x_part1(xq["next"])
            xq["pend"] = xq["next"]
            xq["next"] += 1

    mmrr = 0

    def r0_tile(i):
        nonlocal mmrr
        s2 = i % 2
        if i == 0:
            P.dma(DMA(xt[0], xnT_v[:, :, 0:512]), ds_xt[0], reads=[XN[0]], writes=[B_xt[0]])
        if i + 1 < 8:
            P.dma(DMA(xt[1 - s2], xnT_v[:, :, (i + 1) * 512:(i + 2) * 512]), ds_xt[1 - s2], reads=[XN[i + 1]], writes=[B_xt[1 - s2]])
        ng = 0
        for sub in range(4):
            for cg in range(6):
                bk = mmrr % 4
                mmrr += 1
                for kc in range(8):
                    P.op("pe", MM(pb[bk], xt[s2][:, kc, sub * 128:(sub + 1) * 128], wkv[:, kc, cg * 512:(cg + 1) * 512],
                                  kc == 0, kc == 7), reads=[B_xt[s2], B_wkv], writes=[PB[bk]])
                if cg < 2:
                    for hh in range(2):
                        h = cg * 2 + hh
                        P.op("dve", TS(Kz[s2][:, sub, h * 256:(h + 1) * 256], pb[bk][:, hh * 256:(hh + 1) * 256],
                                       zeta512[:, sub, h:h + 1], None, ALU.mult),
                             reads=[PB[bk], B_const], writes=[B_Kz[s2]])
                else:
                    evac_copy(Vr[s2][:, sub, (cg - 2) * 512:(cg - 1) * 512], pb[bk], [PB[bk]], [B_Vr[s2]])
                ng += 1
        for hc in range(8):
            h = hc // 2
            bk = 6 + hc % 2
            for sub in range(4):
                P.op("pe", MM(pb[bk], Kz[s2][:, sub, hc * 128:(hc + 1) * 128], Vr[s2][:, sub, h * 512:(h + 1) * 512],
                              sub == 0, sub == 3), reads=[B_Kz[s2], B_Vr[s2]], writes=[PB[bk]])
            P.op("dve", STT(S32[:, hc, :], S32[:, hc, :], _HC["g512"][h], pb[bk], ALU.mult, ALU.add),
                 reads=[PB[bk], B_S32[hc]], writes=[B_S32[hc]])

    def x_b1(b):
        for tt in range(8 * b, 8 * b + 8):
            x_part1(tt)

    def x_b2(b):
        for tt in range(8 * b, 8 * b + 8):
            x_part2(tt)

    x_b1(0)
    x_b2(0)
    x_b1(1)
    for i in range(8):
        r0_tile(i)
        if i + 1 < 8:
            x_b2(i + 1)
        if 1 <= i <= 4:
            watt_cast(i - 1)
        if i + 2 < 8:
            x_b1(i + 2)
    for hc in range(8):
        P.op("act", ACP(Sbf[:, hc, :], S32[:, hc, :]), reads=[B_S32[hc]], writes=[B_Sbf[hc]])
    P.barrier()
    if debug == "R0":
        dsd = P.dsem("dbg")
        P.dma(DMA(dbg_s.rearrange("p (a v) -> p a v", a=8), S32), dsd, reads=B_S32, writes=[Buf()])
        return finish()

    C = Carver(SBASE)
    W_A = C([3, 8, 3, 128], BF16)
    B_WA = Buf("W_A")
    ds_WA = P.dsem("W_A")
    abias = [C([3, 256], F32) for _ in range(2)]
    B_ab = [Buf("abias0"), Buf("abias1")]
    ds_ab = [P.dsem("abias0"), P.dsem("abias1")]
    xs = [C([8, 2048], BF16) for _ in range(2)]
    B_xs = [Buf() for _ in range(2)]
    ds_xs = [P.dsem("xs%d" % i) for i in range(2)]
    Kt = [[C([16, 128], BF16) for _ in range(2)] for _ in range(3)]
    Vt = [[C([16, 128], BF16) for _ in range(2)] for _ in range(3)]
    Qt = [C([16, 128], BF16) for _ in range(3)]
    B_Kt = [[Buf() for _ in range(2)] for _ in range(3)]
    B_Vt = [[Buf() for _ in range(2)] for _ in range(3)]
    B_Qt = [Buf() for _ in range(3)]
    sq_all = C([4, 512], BF16)
    sq = [sq_all[:, i, :] for i in range(4)]
    B_sq = [Buf() for _ in range(4)]
    oaT = sq_all.rearrange("p a b -> p (a b)")
    rin = [C([512], F32) for _ in range(2)]
    B_rin = [Buf() for _ in range(2)]
    sb32 = [C([256], F32) for _ in range(4)]
    B_sb = [Buf() for _ in range(4)]
    pT = [C([256], BF16) for _ in range(4)]
    B_pT = [Buf() for _ in range(4)]
    acc = C([2, 2048], F32)
    B_acc = Buf("acc")
    ds_oaT = P.dsem("oaT")
    OA = [[Buf() for _ in range(2)] for _ in range(4)]
    pairs = [(j, s) for j in range(4 if debug != "A1" else 1) for s in (1, 2, 3)]

    W_A_flat = W_A.rearrange("p a b c d -> p (a b c d)")

    def a_load_w(j):
        P.dma(DMA(W_A_flat, w_att_b[j]), ds_WA, reads=[WATT[j]], writes=[B_WA])
        P.dma(DMA(abias[j % 2], c_abias.rearrange("p (g j) c -> p g j c", g=3)[:, :, j, :]), ds_ab[j % 2], writes=[B_ab[j % 2]])

    def a_load_x(pi):
        s_ = pairs[pi][1]
        xl_ = pi % 2
        P.dma(DMA(xs[xl_], xnT_v[:, :, s_ * 2048:(s_ + 1) * 2048]), ds_xs[xl_],
              reads=[XN[4 * s_ + i] for i in range(4)], writes=[B_xs[xl_]])

    a_load_w(0)
    a_load_x(0)
    for pi, (j, s) in enumerate(pairs):
        halo = (s == 1)
        h = s % 2
        X = xs[pi % 2]
        B_X = B_xs[pi % 2]
        AB = abias[j % 2]
        B_AB = B_ab[j % 2]
        items = []
        vgroups = []
        for g in range(3):
            Dg = DIL[g]
            tss = [3] if (halo and g < 2) else [0, 1, 2, 3]
            for ts in tss:
                items.append((g, "k", ts))
                if not halo:
                    items.append((g, "q", ts))
            tiles = list(range(16 - Dg, 16)) if (halo and g < 2) else list(range(16))
            for b0 in range(0, len(tiles), 4):
                vgroups.append((g, tiles[b0:b0 + 4]))

        def stage0(ii):
            g, kind, ts = items[ii]
            bk = ii % 4
            i3 = 0 if kind == "q" else 1
            for kc in range(8):
                P.op("pe", MM(pb[bk], W_A[:, i3, kc, g, :], X[:, kc, ts * 512:(ts + 1) * 512], kc == 0, kc == 7),
                     reads=[B_WA, B_X], writes=[PB[bk]])
            P.op("act", ACTF(sq[bk], pb[bk], AF.Square), reads=[PB[bk]], writes=[B_sq[bk]])

        def stage1(ii):
            g, kind, ts = items[ii]
            Dg = DIL[g]
            hg = g * 4 + j
            bk = ii % 4
            qb = 4 + ii % 2
            r2 = ii % 2
            P.op("pe", MM(pb[qb], ones, sq[bk]), reads=[B_sq[bk], B_const], writes=[PB[qb]])
            P.op("act", ACTF(rin[r2], pb[qb], AF.Ln, bias=epsc, scale=1.0 / 128), reads=[PB[qb], B_const], writes=[B_rin[r2]])
            P.op("act", ACTF(rin[r2], rin[r2], AF.Exp, scale=-0.5), reads=[B_rin[r2]], writes=[B_rin[r2]])
            if kind == "q":
                dst_t, B_dst, gv = Qt[g], B_Qt[g], gq
            else:
                dst_t, B_dst, gv = Kt[g][h], B_Kt[g][h], gk
            if Dg == 1:
                dv = dst_t[:, ts * 4:(ts + 1) * 4, :]
                sv = pb[bk].rearrange("p (a l) -> p a l", a=4)
                rv = rin[r2].rearrange("p (a l) -> p a l", a=4)
            elif Dg == 4:
                dv = dst_t[:, ts * 4:(ts + 1) * 4, :]
                sv = pb[bk].rearrange("p (l r) -> p r l", r=4)
                rv = rin[r2].rearrange("p (l r) -> p r l", r=4)
            else:
                dv = dst_t[:, :, ts * 32:(ts + 1) * 32]
                sv = pb[bk].rearrange("p (l r) -> p r l", r=16)
                rv = rin[r2].rearrange("p (l r) -> p r l", r=16)
            P.op("dve", STT(dv, sv, gv[:, hg:hg + 1], rv, ALU.mult, ALU.mult),
                 reads=[PB[bk], B_rin[r2], B_const], writes=[B_dst])

        def vgroup(vi):
            g, grp_t = vgroups[vi]
            Dg = DIL[g]
            bk = 6 + vi % 2
            for ti, t in enumerate(grp_t):
                s_, r = t // Dg, t % Dg
                base = s_ * 128 * Dg + r
                for kc in range(8):
                    lv = X[:, kc, base:base + 127 * Dg + 1:Dg] if Dg > 1 else X[:, kc, base:base + 128]
                    P.op("pe", MM(pb[bk][:, ti * 128:(ti + 1) * 128], lv, W_A[:, 2, kc, g, :], kc == 0, kc == 7),
                         reads=[B_WA, B_X], writes=[PB[bk]])
            n = len(grp_t)
            evac_copy(Vt[g][h][:, grp_t[0]:grp_t[0] + n, :],
                      pb[bk][:, 0:n * 128].rearrange("p (a d) -> p a d", a=n), [PB[bk]], [B_Vt[g][h]])

        SK = 2
        vi = 0
        for step in range(len(items) + SK):
            if step < len(items):
                stage0(step)
            if vi < len(vgroups):
                vgroup(vi)
                vi += 1
            if step - SK >= 0:
                stage1(step - SK)
        while vi < len(vgroups):
            vgroup(vi)
            vi += 1
        if pi + 1 < len(pairs):
            if pairs[pi + 1][0] != j:
                a_load_w(pairs[pi + 1][0])
            a_load_x(pi + 1)
        issue_casts(2)
        if halo:
            continue
        units = [(g, t) for g in range(3) for t in range(16)]
        info = {}

        def front(ui):
            g, t = units[ui]
            Dg = DIL[g]
            bk = ui % 4
            if t - Dg >= 0:
                kp, vp, B_kp, B_vp, onp = Kt[g][h][:, t - Dg, :], Vt[g][h][:, t - Dg, :], B_Kt[g][h], B_Vt[g][h], ones
            else:
                tp = 16 + t - Dg
                kp, vp, B_kp, B_vp = Kt[g][1 - h][:, tp, :], Vt[g][1 - h][:, tp, :], B_Kt[g][1 - h], B_Vt[g][1 - h]
                onp = ones_h if s == 2 else ones
            kc_, vc_ = Kt[g][h][:, t, :], Vt[g][h][:, t, :]
            qv = Qt[g][:, t, :]
            P.op("pe", MM(pb[bk][:, 0:128], kp, qv), reads=[B_kp, B_Qt[g]], writes=[PB[bk]])
            P.op("pe", MM(pb[bk][:, 128:256], kc_, qv), reads=[B_Kt[g][h], B_Qt[g]], writes=[PB[bk]])
            P.op("dve", TT(sb32[bk], pb[bk][:, 0:256], AB[:, g, :], ALU.add), reads=[PB[bk], B_AB], writes=[B_sb[bk]])
            P.op("act", ACTF(pT[bk], sb32[bk], AF.Exp), reads=[B_sb[bk]], writes=[B_pT[bk]])
            info[ui] = (vp, vc_, onp, B_vp)

        def back(ui):
            g, t = units[ui]
            Dg = DIL[g]
            bk = ui % 4
            vp, vc_, onp, B_vp = info.pop(ui)
            P.op("pe", MM(pb[bk][:, 256:384], vp, pT[bk][:, 0:128], True, False), reads=[B_vp, B_pT[bk]], writes=[PB[bk]])
            P.op("pe", MM(pb[bk][:, 256:384], vc_, pT[bk][:, 128:256], False, True), reads=[B_Vt[g][h], B_pT[bk]], writes=[PB[bk]])
            P.op("pe", MM(pb[bk][:, 384:512], onp, pT[bk][:, 0:128], True, False), reads=[B_const, B_pT[bk]], writes=[PB[bk]])
            P.op("pe", MM(pb[bk][:, 384:512], ones, pT[bk][:, 128:256], False, True), reads=[B_const, B_pT[bk]], writes=[PB[bk]])
            s_, r = t // Dg, t % Dg
            base = s_ * 128 * Dg + r
            av = acc[:, :, base:base + 127 * Dg + 1:Dg] if Dg > 1 else acc[:, :, base:base + 128]
            pv = pb[bk][:, 256:512].rearrange("p (a q) -> p a q", a=2)
            if g == 0:
                P.op("dve", CP(av, pv), reads=[PB[bk]], writes=[B_acc])
            else:
                P.op("dve", TT(av, pv, av, ALU.add), reads=[PB[bk], B_acc], writes=[B_acc])

        for step in range(len(units) + SK):
            if step < len(units):
                front(step)
            if step - SK >= 0:
                back(step - SK)
        P.op("dve", RCP(acc[:, 1, :], acc[:, 1, :]), reads=[B_acc], writes=[B_acc])
        P.op("dve", TT(oaT, acc[:, 0, :], acc[:, 1, :], ALU.mult), reads=[B_acc], writes=B_sq)
        P.dma(DMA(oaT_d[j * 128:(j + 1) * 128, (s - 2) * 2048:(s - 1) * 2048], oaT), ds_oaT,
              reads=B_sq, writes=[OA[j][s - 2]])
    issue_casts(100)
    P.barrier()
    if debug in ("A", "A1"):
        return finish()

    C = Carver(SBASE)
    xtl = C([4, D], F32)
    B_xtl = Buf("xtl")
    ds_xtl = P.dsem("xtl")
    ds_x1 = P.dsem("x1st")
    xnt = [C([8, 512], BF16) for _ in range(2)]
    B_xnt = [Buf() for _ in range(2)]
    ds_xnt = [P.dsem("xnt%d" % i) for i in range(2)]
    oat = [C([4, 512], BF16) for _ in range(2)]
    B_oat = [Buf() for _ in range(2)]
    ds_oat = [P.dsem("oat%d" % i) for i in range(2)]
    NW = 4
    wbuf = [C([4096], BF16) for _ in range(NW)]
    B_wb = [Buf() for _ in range(NW)]
    ds_wb = [P.dsem("wb%d" % i) for i in range(NW)]
    qkT = C([16, 512], BF16)
    B_qT, B_kT = Buf("qT"), Buf("kT")
    KzB = C([4, 1024], BF16)
    B_KzB = Buf("KzB")
    VrB = C([4, 2048], BF16)
    B_VrB = Buf("VrB")
    gT = C([16, 512], BF16)
    B_gT = Buf("gT")
    oh = [C([512], BF16) for _ in range(4)]
    B_oh = [Buf() for _ in range(4)]
    PB7H = [Buf("pb7a"), Buf("pb7b")]
    orT = C([16, 512], BF16)
    B_orT = Buf("orT")
    pTr = [C([4, 128], BF16) for _ in range(2)]
    B_pTr = [Buf() for _ in range(2)]
    dpm = C([4, 128], F32)
    xibc = C([4, 512], F32)
    B_rc = Buf("retconst")
    ds_rc = newds()
    P.dma(DMA(dpm, c_dpm), ds_rc, writes=[B_rc])
    P.dma(DMA(xibc, c_xibc), ds_rc, writes=[B_rc])
    stats = [C([4, 6], F32) for _ in range(2)]
    mv = [C([4, 2], F32) for _ in range(2)]
    rstd = [C([4], F32) for _ in range(2)]
    nmr = [C([4], F32) for _ in range(2)]
    B_st = [[Buf() for _ in range(4)] for _ in range(2)]
    B_rs = [Buf() for _ in range(2)]
    B_nm = [Buf() for _ in range(2)]
    t32 = [C([4, 128], F32) for _ in range(2)]
    B_t32 = [Buf() for _ in range(2)]
    ya32 = [C([512], F32) for _ in range(2)]
    B_ya = [Buf() for _ in range(2)]
    yb32 = [C([512], F32) for _ in range(2)]
    B_yb = [Buf() for _ in range(2)]
    sgT = qkT
    yT = KzB.rearrange("p a b -> p (a b)").rearrange("p (a b) -> p a b", a=8)
    X1 = [Buf("x1_%d" % i) for i in range(8)]
    wa_v = wa_b.rearrange("(k p) c -> p k c", p=128)
    wb_v = wb_b.rearrange("(k p) c -> p k c", p=128)
    wout_v = wout_b.rearrange("(k p) c -> p k c", p=128)
    wcnt = 0
    bkrr = 0
    wav = C([4, 1024], BF16)
    B_wa = Buf("wa_s")
    P.dma(DMA(wav, wa_v), newds(), reads=[WB["wa"]], writes=[B_wa])

    wpre = {}

    def wload(src_v, KC, c0, ncols, wbuf_name, prefetch=False):
        nonlocal wcnt
        key = (wbuf_name, c0)
        if not prefetch and key in wpre:
            return wpre.pop(key)
        sl = wcnt % NW
        wcnt += 1
        view = wbuf[sl].rearrange("p (k c) -> p k c", k=KC)
        P.dma(DMA(view, src_v[:, :, c0:c0 + ncols]), ds_wb[sl], reads=[WB[wbuf_name if wbuf_name != "w_in" else "w_in_rest"]], writes=[B_wb[sl]])
        if prefetch:
            wpre[key] = (view, B_wb[sl])
        return view, B_wb[sl]

    def wload_wb(og_):
        nonlocal wcnt
        sl = wcnt % NW
        wcnt += 1
        P.dma(DMA(wbuf[sl], wb_t[og_]), ds_wb[sl], reads=[WB["wb"]], writes=[B_wb[sl]])
        return wbuf[sl].rearrange("p (k c) -> p k c", k=16), B_wb[sl]

    def tile_loads(i_):
        tok0_ = i_ * 512
        s2_ = i_ % 2
        P.dma(DMA(xnt[s2_], xnT_v[:, :, HALF + tok0_:HALF + tok0_ + 512]), ds_xnt[s2_], reads=[XN[8 + i_]], writes=[B_xnt[s2_]])
        P.dma(DMA(oat[s2_], oaT_d.rearrange("(k p) t -> p k t", p=128)[:, :, tok0_:tok0_ + 512]), ds_oat[s2_],
              reads=[OA[jj][i_ // 4] for jj in range(4)], writes=[B_oat[s2_]])

    tile_loads(0)

    def nextbank():
        nonlocal bkrr
        b = bkrr % 3
        bkrr += 1
        return b

    import os
    KN = os.environ.get('KN', '')
    for i in range(8 if 'T1' not in KN else 1):
        tok0 = i * 512
        s2 = i % 2
        XT, B_XT = xnt[s2], B_xnt[s2]

        def fm_group(c0, evac):
            wv, B_w = wload(w_in_v, 8, c0, 512, "w_in")
            for blk in range(4):
                bk = nextbank()
                for kc in range(8):
                    P.op("pe", MM(pb[bk], wv[:, kc, blk * 128:(blk + 1) * 128], XT[:, kc, :], kc == 0, kc == 7),
                         reads=[B_w, B_XT], writes=[PB[bk]])
                evac(blk, pb[bk], PB[bk])

        for grp in range(2):
            def ev_q(blk, ps, PBb, grp=grp):
                hc = grp * 4 + blk
                P.op("dve", TT(qkT[:, hc, :], ps, xibc[:, hc // 2, :], ALU.mult), reads=[PBb, B_rc], writes=[B_qT])
            fm_group(C_QR + grp * 512, ev_q)
        for grp in range(2):
            def ev_k(blk, ps, PBb, grp=grp):
                hc = grp * 4 + blk
                P.op("act", ACP(qkT[:, 8 + hc, :], ps), reads=[PBb], writes=[B_kT])
            fm_group(C_KR + grp * 512, ev_k)
        for sub in range(4):
            for hc in range(8):
                P.op("pe", TR(pbh[7][:, hc * 128:(hc + 1) * 128], qkT[:, 8 + hc, sub * 128:(sub + 1) * 128], ident),
                     reads=[B_kT, B_const], writes=[PB[7]])
            for h in range(4):
                P.op("dve", TS(KzB[:, sub, h * 256:(h + 1) * 256], pbh[7][:, h * 256:(h + 1) * 256],
                               zeta16[:, h:h + 1], None, ALU.mult), reads=[PB[7], B_const], writes=[B_KzB])
        for grp in range(4):
            wv, B_w = wload(w_in_v, 8, C_VR + grp * 512, 512, "w_in")
            for sub in range(4):
                bk = nextbank()
                for kc in range(8):
                    P.op("pe", MM(pb[bk], XT[:, kc, sub * 128:(sub + 1) * 128], wv[:, kc, :], kc == 0, kc == 7),
                         reads=[B_w, B_XT], writes=[PB[bk]])
                evac_copy(VrB[:, sub, grp * 512:(grp + 1) * 512], pb[bk], [PB[bk]], [B_VrB])
        for grp in range(4):
            def ev_g(blk, ps, PBb, grp=grp):
                P.op("act", ACTF(gT[:, grp * 4 + blk, :], ps, AF.Silu), reads=[PBb], writes=[B_gT])
            fm_group(C_GR + grp * 512, ev_g)
        def emit_A(sub):
            tsl_ = slice(sub * 128, (sub + 1) * 128)
            pr_ = sub % 2
            for h in range(4):
                for c in range(2):
                    hc = h * 2 + c
                    P.op("pe", MM(pb[0][:, h * 128:(h + 1) * 128], qkT[:, 8 + hc, tsl_], qkT[:, hc, tsl_], c == 0, c == 1),
                         reads=[B_kT, B_qT], writes=[PB[0]])
            P.op("dve", TT(pTr[pr_], pb[0].rearrange("p (h q) -> p h q", h=4), dpm, ALU.mult),
                 reads=[PB[0], B_rc], writes=[B_pTr[pr_]])

        emit_A(0)
        for sub in range(4):
            tsl = slice(sub * 128, (sub + 1) * 128)
            pr = sub % 2
            if sub > 0 and 'NOA' in KN:
                emit_A(sub)
            for h in range(4):
                ob = 1 + h
                P.op("pe", MM(pb[ob], pTr[pr][:, h, :], VrB[:, sub, h * 512:(h + 1) * 512], True, False),
                     reads=[B_pTr[pr], B_VrB], writes=[PB[ob]])
                for c in range(2):
                    hc = h * 2 + c
                    P.op("pe", MM(pb[ob], qkT[:, hc, tsl], Sbf[:, hc, :], False, c == 1),
                         reads=[B_qT, B_Sbf[hc]], writes=[PB[ob]])
            for h in range(4):
                ob = 1 + h
                P.op("dve", lambda e, o_=stats[pr][:, h, :], i_=pb[ob]: e.bn_stats(out=o_, in_=i_),
                     reads=[PB[ob]], writes=[B_st[pr][h]])
                P.op("dve", lambda e, o_=mv[pr][:, h, :], i_=stats[pr][:, h, :]: e.bn_aggr(out=o_, in_=i_),
                     reads=[B_st[pr][h]], writes=[B_st[pr][h]])
            P.op("act", ACTF(rstd[pr], mv[pr][:, :, 1], AF.Ln, bias=epsc, scale=1.0),
                 reads=B_st[pr] + [B_const], writes=[B_rs[pr]])
            P.op("act", ACTF(rstd[pr], rstd[pr], AF.Exp, scale=-0.5), reads=[B_rs[pr]], writes=[B_rs[pr]])
            P.op("dve", STT(nmr[pr], mv[pr][:, :, 0], -1.0, rstd[pr], ALU.mult, ALU.mult),
                 reads=B_st[pr] + [B_rs[pr]], writes=[B_nm[pr]])
            for h in range(4):
                ob = 1 + h
                P.op("act", ACTF(oh[h], pb[ob], AF.Identity, bias=nmr[pr][:, h:h + 1], scale=rstd[pr][:, h:h + 1]),
                     reads=[PB[ob], B_rs[pr], B_nm[pr]], writes=[B_oh[h]])
            for hc in range(8):
                h = hc // 2
                sbk = 5 if hc % 2 else 0
                P.op("pe", MM(pb[sbk], KzB[:, sub, hc * 128:(hc + 1) * 128], VrB[:, sub, h * 512:(h + 1) * 512]),
                     reads=[B_KzB, B_VrB], writes=[PB[sbk]])
                P.op("dve", STT(S32[:, hc, :], S32[:, hc, :], _HC["g128"][h], pb[sbk], ALU.mult, ALU.add),
                     reads=[PB[sbk], B_S32[hc]], writes=[B_S32[hc]])
                P.op("act", ACP(Sbf[:, hc, :], S32[:, hc, :]), reads=[B_S32[hc]], writes=[B_Sbf[hc]])
            if sub < 3 and 'NOA' not in KN:
                emit_A(sub + 1)
            for h in range(4):
                tb_ = 6 + h % 2
                for fc in range(4):
                    P.op("pe", TR(pbh[tb_][:, fc * 128:(fc + 1) * 128], oh[h][:, fc * 128:(fc + 1) * 128], ident),
                         reads=[B_oh[h], B_const], writes=[PB[tb_]])
                for fc in range(4):
                    f = h * 4 + fc
                    if fc < 3:
                        P.op("act", ACTF(t32[h % 2][:, fc, :], pbh[tb_][:, fc * 128:(fc + 1) * 128], AF.Identity,
                                         bias=gnb[:, f:f + 1], scale=gng[:, f:f + 1]),
                             reads=[PB[tb_], B_const], writes=[B_t32[h % 2]])
                    else:
                        P.op("dve", TS(t32[h % 2][:, fc, :], pbh[tb_][:, fc * 128:(fc + 1) * 128],
                                       gng[:, f:f + 1], gnb[:, f:f + 1], ALU.mult, ALU.add),
                             reads=[PB[tb_], B_const], writes=[B_t32[h % 2]])
                P.op("pool" if h < 3 else "dve", TT(orT[:, h * 4:(h + 1) * 4, tsl], t32[h % 2], gT[:, h * 4:(h + 1) * 4, tsl], ALU.mult),
                     reads=[B_t32[h % 2], B_gT], writes=[B_orT])
        P.dma(DMA(xtl, x_main[tok0:tok0 + 512, :].rearrange("(s p) f -> p s f", p=128)), ds_xtl, writes=[B_xtl])
        for grp in range(4):
            def ev_s(blk, ps, PBb, grp=grp):
                P.op("act", ACTF(sgT[:, grp * 4 + blk, :], ps, AF.Sigmoid), reads=[PBb], writes=[B_qT, B_kT])
            fm_group(C_GA + grp * 512, ev_s)
        for og in range(4):
            wbv, B_wbv = wload_wb(og)
            for o2 in range(2):
                oc = og * 2 + o2
                y2 = oc % 2
                bka = nextbank()
                for kc in range(4):
                    P.op("pe", MM(pb[bka], wav[:, kc, oc * 128:(oc + 1) * 128], oat[s2][:, kc, :], kc == 0, kc == 3),
                         reads=[B_wa, B_oat[s2]], writes=[PB[bka]])
                P.op("dve", TT(ya32[y2], pb[bka], sgT[:, oc, :], ALU.mult), reads=[PB[bka], B_qT, B_kT], writes=[B_ya[y2]])
                bkb = nextbank()
                for kc in range(16):
                    P.op("pe", MM(pb[bkb], wbv[:, kc, o2 * 128:(o2 + 1) * 128], orT[:, kc, :], kc == 0, kc == 15),
                         reads=[B_wbv, B_orT], writes=[PB[bkb]])
                P.op("dve", TT(yb32[y2], pb[bkb], sgT[:, 8 + oc, :], ALU.mult), reads=[PB[bkb], B_qT, B_kT], writes=[B_yb[y2]])
                P.op("pool", TT(yT[:, oc, :], ya32[y2], yb32[y2], ALU.add), reads=[B_ya[y2], B_yb[y2]], writes=[B_KzB])
        if dbg and i == 0:
            dsd = P.dsem("dbg")
            P.dma(DMA(dbg_or.rearrange("p (a v) -> p a v", a=16), orT), dsd, reads=[B_orT], writes=[Buf()])
            P.dma(DMA(dbg_y.rearrange("p (a v) -> p a v", a=8), yT), dsd, reads=[B_KzB], writes=[Buf()])
            P.dma(DMA(dbg_sg.rearrange("p (a v) -> p a v", a=16), sgT), dsd, reads=[B_qT, B_kT], writes=[Buf()])
        for grp in range(2):
            wv, B_w = wload(wout_v, 8, grp * 512, 512, "wout")
            for sub in range(4):
                bk = nextbank()
                for kc in range(8):
                    P.op("pe", MM(pb[bk], yT[:, kc, sub * 128:(sub + 1) * 128], wv[:, kc, :], kc == 0, kc == 7),
                         reads=[B_w, B_KzB], writes=[PB[bk]])
                P.op("dve", TT(xtl[:, sub, grp * 512:(grp + 1) * 512], pb[bk], xtl[:, sub, grp * 512:(grp + 1) * 512], ALU.add),
                     reads=[PB[bk], B_xtl], writes=[B_xtl])
        if i + 1 < 8 and 'T1' not in KN:
            tile_loads(i + 1)
            wload(w_in_v, 8, C_QR, 512, "w_in", prefetch=True)
            wload(w_in_v, 8, C_QR + 512, 512, "w_in", prefetch=True)
        P.dma(DMA(out[tok0:tok0 + 512, :].rearrange("(s p) f -> p s f", p=128), xtl), ds_x1, reads=[B_xtl], writes=[X1[i]])
    P.barrier()
    if debug == "B1":
        return finish()

    C = Carver(GBASE)
    wup_s = C([8, 4096], BF16)
    wdn_s = C([32, 1024], BF16)
    B_wup, B_wdn = Buf("wup"), Buf("wdn")
    ds_w2 = P.dsem("w2")
    ds_w2b = P.dsem("w2b")
    wup_v = wup_b.rearrange("(k p) c -> p k c", p=128)
    wdn_v = wdown_b.rearrange("(k p) c -> p k c", p=128)
    B_wupq = [Buf("wup%d" % q) for q in range(4)]
    ds_wupq = [P.dsem("wupq%d" % q) for q in range(4)]
    def b2_load_weights():
        for q4 in range(4):
            P.dma(DMA(wup_s[:, :, q4 * 1024:(q4 + 1) * 1024], wup_v[:, :, q4 * 1024:(q4 + 1) * 1024]), ds_wupq[q4],
                  reads=[WB["wup"]], writes=[B_wupq[q4]])
        for q4 in range(4):
            P.dma(DMA(wdn_s[:, q4 * 8:(q4 + 1) * 8, :], wdn_v[:, q4 * 8:(q4 + 1) * 8, :]), ds_w2b,
                  reads=[WB["wdown"]], writes=[B_wdn])

    g2bc = C([1, D], F32)
    B_g2 = Buf("g2bc")
    P.dma(DMA(g2bc, norm2_g.partition_broadcast(128)), newds(), writes=[B_g2])
    x1t = [C([2, D], F32) for _ in range(3)]
    B_x1t = [Buf() for _ in range(3)]
    ds_x1t = [P.dsem("x1t%d" % i) for i in range(3)]
    ds_ot = [P.dsem("ot%d" % i) for i in range(3)]
    junk2 = C([D], BF16)
    B_junk2 = Buf()
    ss2 = [C([1], F32) for _ in range(2)]
    B_ss2 = [Buf() for _ in range(2)]
    xnb2 = [C([D], BF16) for _ in range(2)]
    B_xnb2 = [Buf() for _ in range(2)]
    xn2T = [C([8, 256], BF16) for _ in range(2)]
    B_xn2T = [Buf() for _ in range(2)]
    hT = [C([32, 256], BF16) for _ in range(2)]
    B_hT = [Buf("hT0"), Buf("hT1")]
    rl = [C([256], BF16) for _ in range(2)]
    B_rl = [Buf() for _ in range(2)]
    OUTB = [Buf() for _ in range(16)]

    def b2_norm1(i):
        tok0 = i * 256
        s3 = i % 3
        P.dma(DMA(x1t[s3], out[tok0:tok0 + 256, :].rearrange("(s p) f -> p s f", p=128)), ds_x1t[s3],
              reads=[X1[i // 2]], writes=[B_x1t[s3]])
        for sub in range(2):
            xi_ = x1t[s3][:, sub, :]
            P.op("act", ACTF(junk2, xi_, AF.Square, accum_out=ss2[sub]), reads=[B_x1t[s3]], writes=[B_junk2, B_ss2[sub]])
            P.op("act", ACTF(ss2[sub], ss2[sub], AF.Ln, bias=epsc, scale=1.0 / D), reads=[B_ss2[sub], B_const], writes=[B_ss2[sub]])
            P.op("act", ACTF(ss2[sub], ss2[sub], AF.Exp, scale=-0.5), reads=[B_ss2[sub]], writes=[B_ss2[sub]])
            P.op("dve", STT(xnb2[sub], xi_, ss2[sub], g2bc[:, 0, :], ALU.mult, ALU.mult),
                 reads=[B_x1t[s3], B_ss2[sub], B_g2], writes=[B_xnb2[sub]])

    def b2_norm2(i):
        s2 = i % 2
        for sub in range(2):
            for kc in range(8):
                P.op("pe", TR(pbh[sub][:, kc * 128:(kc + 1) * 128], xnb2[sub][:, kc * 128:(kc + 1) * 128], ident),
                     reads=[B_xnb2[sub], B_const], writes=[PB[sub]])
            evac_copy(xn2T[s2][:, :, sub * 128:(sub + 1) * 128], pbh[sub][:, 0:1024].rearrange("p (k t) -> p k t", k=8),
                      [PB[sub]], [B_xn2T[s2]])

    def b2_up(i):
        s2 = i % 2
        for fc in range(32):
            bk = 2 + fc % 3
            for kc in range(8):
                P.op("pe", MM(pb[bk][:, 0:256], wup_s[:, kc, fc * 128:(fc + 1) * 128], xn2T[s2][:, kc, :], kc == 0, kc == 7),
                     reads=[B_wupq[fc // 8], B_xn2T[s2]], writes=[PB[bk]])
            r2 = fc % 2
            P.op("act", ACTF(rl[r2], pb[bk][:, 0:256], AF.Relu), reads=[PB[bk]], writes=[B_rl[r2]])
            P.op("pool", TT(hT[s2][:, fc, :], rl[r2], rl[r2], ALU.mult), reads=[B_rl[r2]], writes=[B_hT[s2]])

    def b2_down(i):
        tok0 = i * 256
        s2 = i % 2
        s3 = i % 3
        for sub in range(2):
            for grp in range(2):
                bk = 5 + (sub * 2 + grp) % 3
                for kc in range(32):
                    P.op("pe", MM(pb[bk], hT[s2][:, kc, sub * 128:(sub + 1) * 128], wdn_s[:, kc, grp * 512:(grp + 1) * 512],
                                  kc == 0, kc == 31), reads=[B_hT[s2], B_wdn], writes=[PB[bk]])
                P.op("dve", TT(x1t[s3][:, sub, grp * 512:(grp + 1) * 512], pb[bk], x1t[s3][:, sub, grp * 512:(grp + 1) * 512],
                               ALU.add), reads=[PB[bk], B_x1t[s3]], writes=[B_x1t[s3]])
        P.dma(DMA(out[tok0:tok0 + 256, :].rearrange("(s p) f -> p s f", p=128), x1t[s3]), ds_ot[s3],
              reads=[B_x1t[s3]], writes=[OUTB[i]])

    b2_norm1(0)
    b2_load_weights()
    b2_norm2(0)
    for i in range(16):
        b2_up(i)
        if i + 1 < 16:
            b2_norm1(i + 1)
        if i >= 1:
            b2_down(i - 1)
        if i + 1 < 16:
            b2_norm2(i + 1)
    b2_down(15)
    P.wait_bufs("sp", OUTB)
    return finish()


def _in_maps(inputs):
    x = np.ascontiguousarray(inputs["x"], dtype=np.float32)
    shared = {
        "norm1_g": inputs["norm1_g"].reshape(1, D), "norm2_g": inputs["norm2_g"].reshape(1, D),
        "w_in": inputs["w_in"].reshape(D, IN_W), "q_norm_g": inputs["q_norm_g"].reshape(12, 128),
        "k_norm_g": inputs["k_norm_g"].reshape(12, 128), "ret_gn_g": inputs["ret_gn_g"].reshape(1, 2048),
        "ret_gn_b": inputs["ret_gn_b"].reshape(1, 2048), "w_proj_a": inputs["w_proj_a"].reshape(512, D),
        "w_proj_b": inputs["w_proj_b"].reshape(2048, D), "w_out": inputs["w_out"].reshape(D, D),
        "w_up": inputs["w_up"].reshape(D, 4096), "w_down": inputs["w_down"].reshape(4096, D),
        "c_ident": _HC["ident"], "c_abias": _HC["abias"], "c_dpm": _HC["dpm"], "c_xibc": _HC["xibc"],
        "c_zeta16": _HC["zeta16"], "c_zeta512": _HC["zeta512"],
    }
    shared = {k: np.ascontiguousarray(v, dtype=np.float32) for k, v in shared.items()}
    maps = []
    zeros = np.zeros((HALF, D), np.float32)
    for c in range(8):
        b, half = c // 2, c % 2
        m = dict(shared)
        m["x_main"] = x[b, half * HALF:(half + 1) * HALF]
        m["x_prev"] = x[b, 0:HALF] if half else zeros
        m["hflag"] = np.full((128, 128), float(half), np.float32)
        maps.append(m)
    return maps


def kernel(**inputs):
    nc = build_nc()
    maps = _in_maps(inputs)
    res = run_bass_kernel_spmd(nc, maps, core_ids=list(range(8)))
    outp = np.empty((4, 2 * HALF, D), np.float32)
    for c in range(8):
        b, half = c // 2, c % 2
        outp[b, half * HALF:(half + 1) * HALF] = res.results[c]["out"]
    return outp
```
